# Optimizing a Trainium2 kernel written in Bass

```python
import math
import jax
import jax.numpy as jnp
from jax import lax
import numpy as np

D_MODEL = 1024
BATCH = 4
SEQ = 4096
DEPTH = 4
DEC_BATCH = 32
DEC_SEQ = 4
PAST_LEN = 8192
PAGE_SIZE = 128

N_EVEN = (DEPTH + 1) // 2
N_ODD = DEPTH // 2
A_HEADS = 4
A_QK_DIM = 64
A_V_DIM = 2 * A_QK_DIM
A_Q_W = A_HEADS * 2 * A_QK_DIM
A_V_W = A_HEADS * A_V_DIM
B_CH = 512
CONV_W = 31
EVEN_IN = 2 * A_Q_W + A_V_W + 2 * B_CH
EVEN_MIX = A_V_W + B_CH
C_WIDTH = D_MODEL
GROUP_CH = 16
N_GROUPS = C_WIDTH // GROUP_CH
STATE_P = 64
D_FF = 4 * D_MODEL
Q_BLOCK = 128
RMS_EPS = 1e-6
SUBLN_EPS = 1e-5
LN_EPS = 1e-5
DT_MIN = 1e-3
DT_MAX = 1e-1

kernel_name = "hybrid_diffattn_conformer_s5_step"


def rms_norm(x, g, eps=RMS_EPS):
    xf = x.astype(jnp.float32)
    y = xf * lax.rsqrt(jnp.mean(xf * xf, axis=-1, keepdims=True) + eps)
    return (y * g.astype(jnp.float32)).astype(x.dtype)


def layer_norm(x, g, b, eps=LN_EPS):
    xf = x.astype(jnp.float32)
    mu = jnp.mean(xf, axis=-1, keepdims=True)
    xc = xf - mu
    y = xc * lax.rsqrt(jnp.mean(xc * xc, axis=-1, keepdims=True) + eps)
    return y * g.astype(jnp.float32) + b.astype(jnp.float32)


def lambda_init_fn(layer):
    return 0.8 - 0.6 * math.exp(-0.3 * layer)


def diff_attention(q, k, v, q_pos, k_pos, lam):
    scale = A_QK_DIM ** -0.5
    s = jnp.einsum('bqhcd,bkhcd->bhcqk', q.astype(jnp.float32), k.astype(jnp.float32)) * scale
    visible = k_pos[None, :] <= q_pos[:, None]
    s = jnp.where(visible, s, -jnp.inf)
    p = jax.nn.softmax(s, axis=-1)
    w = p[:, :, 0] - lam * p[:, :, 1]
    return jnp.einsum('bhqk,bkhe->bqhe', w, v.astype(jnp.float32))


def diff_attention_blocked(q, k, v, lam):
    bt, L = q.shape[0], q.shape[1]
    nb = L // Q_BLOCK
    qb = q.reshape(bt, nb, Q_BLOCK, A_HEADS, 2, A_QK_DIM).transpose(1, 0, 2, 3, 4, 5)
    pos = jnp.arange(L, dtype=jnp.int32)
    pb = pos.reshape(nb, Q_BLOCK)
    out = lax.map(lambda a: diff_attention(a[0], k, v, a[1], pos, lam), (qb, pb))
    return out.transpose(1, 0, 2, 3, 4).reshape(bt, L, A_HEADS, A_V_DIM)


def causal_depthwise_conv(u, buf, w, b):
    full = jnp.concatenate([buf.astype(u.dtype), u], axis=1)
    y = lax.conv_general_dilated(full, w.astype(u.dtype)[:, None, :], window_strides=(1,), padding='VALID',
                                 dimension_numbers=('NWC', 'WIO', 'NWC'), feature_group_count=B_CH)
    return y + b.astype(u.dtype), full[:, -(CONV_W - 1):]


def even_mixer(xn, w_in, lam_qk, subln_g, conv_w, conv_b, ln_g, ln_b, w_out, layer, k_past, v_past, conv_buf):
    bt, L, _ = xn.shape
    proj = xn @ w_in
    q, k, v, ug = jnp.split(proj, [A_Q_W, 2 * A_Q_W, 2 * A_Q_W + A_V_W], axis=-1)
    q = q.reshape(bt, L, A_HEADS, 2, A_QK_DIM)
    k = k.reshape(bt, L, A_HEADS, 2, A_QK_DIM)
    v = v.reshape(bt, L, A_HEADS, A_V_DIM)
    lam_init = lambda_init_fn(layer)
    lq = lam_qk.astype(jnp.float32)
    lam = jnp.exp(jnp.sum(lq[0] * lq[1])) - jnp.exp(jnp.sum(lq[2] * lq[3])) + lam_init
    if k_past is None:
        attn = diff_attention_blocked(q, k, v, lam)
        conv_buf = jnp.zeros((bt, CONV_W - 1, B_CH), proj.dtype)
    else:
        past_len = k_past.shape[1]
        k_all = jnp.concatenate([k_past.astype(k.dtype), k], axis=1)
        v_all = jnp.concatenate([v_past.astype(v.dtype), v], axis=1)
        q_pos = past_len + jnp.arange(L, dtype=jnp.int32)
        k_pos = jnp.arange(past_len + L, dtype=jnp.int32)
        attn = diff_attention(q, k_all, v_all, q_pos, k_pos, lam)
    attn = rms_norm(attn, subln_g, SUBLN_EPS) * (1.0 - lam_init)
    attn = attn.reshape(bt, L, A_V_W)
    a, g = jnp.split(ug, 2, axis=-1)
    u = a * jax.nn.sigmoid(g)
    c, new_buf = causal_depthwise_conv(u, conv_buf, conv_w, conv_b)
    c = jax.nn.silu(layer_norm(c, ln_g, ln_b))
    mixed = jnp.concatenate([attn.astype(xn.dtype), c.astype(xn.dtype)], axis=-1)
    return mixed @ w_out, k, v, new_buf


def _complex_affine_combine(e1, e2):
    a1r, a1i, b1r, b1i = e1
    a2r, a2i, b2r, b2i = e2
    return (a2r * a1r - a2i * a1i,
            a2r * a1i + a2i * a1r,
            a2r * b1r - a2i * b1i + b2r,
            a2r * b1i + a2i * b1r + b2i)


def s5_mixer(xn, w_in, a_re, a_im, b_re, b_im, c_re, c_im, d_skip, log_dt, w_gate, w_out, h0_re, h0_im):
    f32 = jnp.float32
    bt, L, _ = xn.shape
    u = (xn @ w_in).astype(f32)
    dt = jnp.exp(log_dt.astype(f32))[:, None]
    lr, li = a_re.astype(f32), a_im.astype(f32)
    mag = jnp.exp(lr * dt)
    ang = li * dt
    ab_re, ab_im = mag * jnp.cos(ang), mag * jnp.sin(ang)
    den = lr * lr + li * li
    nr, ni = ab_re - 1.0, ab_im
    f_re = (nr * lr + ni * li) / den
    f_im = (ni * lr - nr * li) / den
    br, bi = b_re.astype(f32), b_im.astype(f32)
    bb_re = f_re[..., None] * br - f_im[..., None] * bi
    bb_im = f_re[..., None] * bi + f_im[..., None] * br
    ug = u.reshape(bt, L, N_GROUPS, GROUP_CH)
    bu_re = jnp.einsum('blgc,gpc->blgp', ug, bb_re)
    bu_im = jnp.einsum('blgc,gpc->blgp', ug, bb_im)
    h0r, h0i = h0_re.astype(f32), h0_im.astype(f32)
    bu_re = bu_re.at[:, 0].add(ab_re * h0r - ab_im * h0i)
    bu_im = bu_im.at[:, 0].add(ab_re * h0i + ab_im * h0r)
    a_re_b = jnp.broadcast_to(ab_re, bu_re.shape)
    a_im_b = jnp.broadcast_to(ab_im, bu_im.shape)
    _, _, h_re, h_im = lax.associative_scan(_complex_affine_combine, (a_re_b, a_im_b, bu_re, bu_im), axis=1)
    y = (jnp.einsum('blgp,gcp->blgc', h_re, c_re.astype(f32))
         - jnp.einsum('blgp,gcp->blgc', h_im, c_im.astype(f32)))
    y = y.reshape(bt, L, C_WIDTH) + d_skip.astype(f32) * u
    z = jax.nn.gelu(y, approximate=False)
    z = z * jax.nn.sigmoid(z @ w_gate.astype(f32))
    out = (z @ w_out.astype(f32)).astype(xn.dtype)
    return out, h_re[:, -1], h_im[:, -1]


def sq_relu_mlp(xn, w_up, w_down):
    return jnp.square(jax.nn.relu(xn @ w_up)) @ w_down


def setup_inputs(seed: int = 0) -> dict:
    key = jax.random.key(seed)
    ks = list(jax.random.split(key, 40))
    f32 = jnp.float32
    nrm = lambda k, shape, s: jax.random.normal(k, shape, f32) * s
    n_pages = PAST_LEN // PAGE_SIZE
    n_used = DEC_BATCH * n_pages
    n_phys = n_used + max(1, n_used // 4)
    page_table = jax.random.permutation(ks[0], n_phys)[:n_used].reshape(DEC_BATCH, n_pages).astype(jnp.int32)
    gain = lambda k, shape: 1.0 + nrm(k, shape, 0.01)
    a_im = math.pi * jnp.arange(STATE_P, dtype=f32)[None, None, :] + nrm(ks[20], (N_ODD, N_GROUPS, STATE_P), 0.01)
    return {
        "x_prompt": nrm(ks[1], (BATCH, SEQ, D_MODEL), 1.0),
        "x_sample": nrm(ks[2], (DEC_BATCH, DEC_SEQ, D_MODEL), 1.0),
        "cache_k": nrm(ks[3], (N_EVEN, n_phys, PAGE_SIZE, A_HEADS, 2, A_QK_DIM), 1.0),
        "cache_v": nrm(ks[4], (N_EVEN, n_phys, PAGE_SIZE, A_HEADS, A_V_DIM), 1.0),
        "page_table": page_table,
        "state_conv": nrm(ks[5], (N_EVEN, DEC_BATCH, CONV_W - 1, B_CH), 0.5),
        "state_ssm_re": nrm(ks[6], (N_ODD, DEC_BATCH, N_GROUPS, STATE_P), 0.1),
        "state_ssm_im": nrm(ks[7], (N_ODD, DEC_BATCH, N_GROUPS, STATE_P), 0.1),
        "g_mix_pre": gain(ks[8], (DEPTH, D_MODEL)),
        "g_mix_post": gain(ks[9], (DEPTH, D_MODEL)),
        "g_ffn_pre": gain(ks[10], (DEPTH, D_MODEL)),
        "g_ffn_post": gain(ks[11], (DEPTH, D_MODEL)),
        "w_in_even": nrm(ks[12], (N_EVEN, D_MODEL, EVEN_IN), D_MODEL ** -0.5),
        "lambda_qk": nrm(ks[13], (N_EVEN, 4, A_QK_DIM), 0.1),
        "subln_g": gain(ks[14], (N_EVEN, A_V_DIM)),
        "conv_w": nrm(ks[15], (N_EVEN, CONV_W, B_CH), CONV_W ** -0.5),
        "conv_b": nrm(ks[16], (N_EVEN, B_CH), 0.01),
        "conv_ln_g": gain(ks[17], (N_EVEN, B_CH)),
        "conv_ln_b": nrm(ks[18], (N_EVEN, B_CH), 0.01),
        "w_out_even": nrm(ks[19], (N_EVEN, EVEN_MIX, D_MODEL), EVEN_MIX ** -0.5),
        "w_in_odd": nrm(ks[21], (N_ODD, D_MODEL, C_WIDTH), D_MODEL ** -0.5),
        "ssm_a_re": -0.5 + nrm(ks[22], (N_ODD, N_GROUPS, STATE_P), 0.01),
        "ssm_a_im": a_im,
        "ssm_b_re": nrm(ks[23], (N_ODD, N_GROUPS, STATE_P, GROUP_CH), (2 * GROUP_CH) ** -0.5),
        "ssm_b_im": nrm(ks[24], (N_ODD, N_GROUPS, STATE_P, GROUP_CH), (2 * GROUP_CH) ** -0.5),
        "ssm_c_re": nrm(ks[25], (N_ODD, N_GROUPS, GROUP_CH, STATE_P), STATE_P ** -0.5),
        "ssm_c_im": nrm(ks[26], (N_ODD, N_GROUPS, GROUP_CH, STATE_P), STATE_P ** -0.5),
        "ssm_d": nrm(ks[27], (N_ODD, C_WIDTH), 1.0),
        "ssm_log_dt": jax.random.uniform(ks[28], (N_ODD, N_GROUPS), f32, math.log(DT_MIN), math.log(DT_MAX)),
        "w_gate_odd": nrm(ks[29], (N_ODD, C_WIDTH, C_WIDTH), C_WIDTH ** -0.5),
        "w_out_odd": nrm(ks[30], (N_ODD, C_WIDTH, D_MODEL), C_WIDTH ** -0.5),
        "w_ffn_up": nrm(ks[31], (DEPTH, D_MODEL, D_FF), D_MODEL ** -0.5),
        "w_ffn_down": nrm(ks[32], (DEPTH, D_FF, D_MODEL), D_FF ** -0.5),
    }


def reference(x_prompt, x_sample, cache_k, cache_v, page_table, state_conv, state_ssm_re, state_ssm_im,
              g_mix_pre, g_mix_post, g_ffn_pre, g_ffn_post,
              w_in_even, lambda_qk, subln_g, conv_w, conv_b, conv_ln_g, conv_ln_b, w_out_even,
              w_in_odd, ssm_a_re, ssm_a_im, ssm_b_re, ssm_b_im, ssm_c_re, ssm_c_im, ssm_d, ssm_log_dt,
              w_gate_odd, w_out_odd, w_ffn_up, w_ffn_down):
    hp, hs = x_prompt, x_sample
    dec_b = page_table.shape[0]
    kp_l, vp_l, ks_l, vs_l, cp_l, cs_l = [], [], [], [], [], []
    srp_l, sip_l, srs_l, sis_l = [], [], [], []
    for l in range(DEPTH):
        i = l // 2
        if l % 2 == 0:
            ew = (w_in_even[i], lambda_qk[i], subln_g[i], conv_w[i], conv_b[i], conv_ln_g[i], conv_ln_b[i], w_out_even[i])
            mix, k_new, v_new, buf_new = even_mixer(rms_norm(hp, g_mix_pre[l]), *ew, l, None, None, None)
            hp = hp + rms_norm(mix, g_mix_post[l])
            kp_l.append(k_new); vp_l.append(v_new); cp_l.append(buf_new)
            k_past = cache_k[i, page_table].reshape(dec_b, -1, A_HEADS, 2, A_QK_DIM)
            v_past = cache_v[i, page_table].reshape(dec_b, -1, A_HEADS, A_V_DIM)
            mix, k_new, v_new, buf_new = even_mixer(rms_norm(hs, g_mix_pre[l]), *ew, l, k_past, v_past, state_conv[i])
            hs = hs + rms_norm(mix, g_mix_post[l])
            ks_l.append(k_new); vs_l.append(v_new); cs_l.append(buf_new)
        else:
            ow = (w_in_odd[i], ssm_a_re[i], ssm_a_im[i], ssm_b_re[i], ssm_b_im[i], ssm_c_re[i], ssm_c_im[i],
                  ssm_d[i], ssm_log_dt[i], w_gate_odd[i], w_out_odd[i])
            zeros = jnp.zeros((hp.shape[0], N_GROUPS, STATE_P), jnp.float32)
            mix, sr, si = s5_mixer(rms_norm(hp, g_mix_pre[l]), *ow, zeros, zeros)
            hp = hp + rms_norm(mix, g_mix_post[l])
            srp_l.append(sr); sip_l.append(si)
            mix, sr, si = s5_mixer(rms_norm(hs, g_mix_pre[l]), *ow, state_ssm_re[i], state_ssm_im[i])
            hs = hs + rms_norm(mix, g_mix_post[l])
            srs_l.append(sr); sis_l.append(si)
        hp = hp + rms_norm(sq_relu_mlp(rms_norm(hp, g_ffn_pre[l]), w_ffn_up[l], w_ffn_down[l]), g_ffn_post[l])
        hs = hs + rms_norm(sq_relu_mlp(rms_norm(hs, g_ffn_pre[l]), w_ffn_up[l], w_ffn_down[l]), g_ffn_post[l])
    new_k_prompt = jnp.stack(kp_l)
    new_v_prompt = jnp.stack(vp_l)
    new_k_sample = jnp.stack(ks_l)
    new_v_sample = jnp.stack(vs_l)
    new_conv_prompt = jnp.stack(cp_l)
    new_conv_sample = jnp.stack(cs_l)
    new_ssm_re_prompt = jnp.stack(srp_l)
    new_ssm_im_prompt = jnp.stack(sip_l)
    new_ssm_re_sample = jnp.stack(srs_l)
    new_ssm_im_sample = jnp.stack(sis_l)
    return (hp, hs, new_k_prompt, new_v_prompt, new_k_sample, new_v_sample, new_conv_prompt, new_conv_sample,
            new_ssm_re_prompt, new_ssm_im_prompt, new_ssm_re_sample, new_ssm_im_sample)
```

```python
import math
import os
import numpy as np
from contextlib import ExitStack
import concourse.bass as bass
import concourse.mybir as mybir
from concourse.bass_utils import run_bass_kernel_spmd

F32 = mybir.dt.float32
BF16 = mybir.dt.bfloat16
I32 = mybir.dt.int32
AF = mybir.ActivationFunctionType
ALU = mybir.AluOpType
AX = mybir.AxisListType

NDMA = 40
PENG = os.environ.get("PENG", "dve")
PENG2 = os.environ.get("PENG2", "pool")
D = 1024
DFF = 4096
TT = 256
PI = math.pi

FULL_CFG = dict(NC=4, NB=4, SEQ=4096, DEC_B=32, NPAGES=64, NPHYS=2560, DEPTH=4)


class T:
    __slots__ = ("w", "r")

    def __init__(self):
        self.w = None
        self.r = {}


class Buf:
    def __init__(self, t, psum=False):
        self.t = t
        self.T = T()
        self.psum = psum

    def __getitem__(self, k):
        return self.t[k]


class Prog:
    CE = ["pe", "act", "dve", "pool"]

    def __init__(self, nc, es):
        self.nc = nc
        self.eng = {"pe": nc.tensor, "act": nc.scalar, "dve": nc.vector, "pool": nc.gpsimd, "sp": nc.sync}
        self.sem = {}
        for e in self.CE:
            self.sem[e] = es.enter_context(nc.semaphore("s_" + e))
        for k in range(NDMA):
            self.sem[("dma", k)] = es.enter_context(nc.semaphore("s_dma%d" % k))
        self.cnt = {k: 0 for k in self.sem}
        self.seen = {e: {} for e in self.eng}
        self.rr = {"sp": 0, "pool": 0}
        self.n_ins = 0

    def emit(self, engine, fn, reads=(), writes=(), inc=True, dma=False):
        if engine != "pe":
            ex = [b for b in reads if b.psum]
            if ex:
                reads = [b for b in reads if not b.psum]
                writes = list(writes) + [b for b in ex if b not in writes]
        waits = {}

        def add(ev):
            if ev is None:
                return
            k, v = ev
            if engine == "pe" and k == "pe":
                return
            if waits.get(k, 0) < v:
                waits[k] = v

        for b in reads:
            add(b.T.w)
        for b in writes:
            add(b.T.w)
            for k, v in b.T.r.items():
                add((k, v))
        seen = self.seen[engine]
        e = self.eng[engine]
        for k, v in waits.items():
            if seen.get(k, 0) >= v:
                continue
            seen[k] = v
            e.wait_ge(self.sem[k], v)
        if dma:
            NSP = 24
            if engine == "sp":
                k = ("dma", self.rr["sp"]); self.rr["sp"] = (self.rr["sp"] + 1) % NSP
            else:
                k = ("dma", NSP + self.rr["pool"]); self.rr["pool"] = (self.rr["pool"] + 1) % (NDMA - NSP)
            if self.cnt[k] > 0 and seen.get(k, 0) < self.cnt[k]:
                seen[k] = self.cnt[k]
                e.wait_ge(self.sem[k], self.cnt[k])
        ins = fn(e)
        self.n_ins += 1
        if dma:
            self.cnt[k] += 16
            ins.then_inc(self.sem[k], 16)
            ev = (k, self.cnt[k])
        elif inc:
            self.cnt[engine] += 1
            ins.then_inc(self.sem[engine], 1)
            ev = (engine, self.cnt[engine])
        else:
            ev = (engine, self.cnt[engine] + 1)
        for b in writes:
            b.T.w = ev
            b.T.r = {}
        for b in reads:
            k, v = ev
            if b.T.r.get(k, 0) < v:
                b.T.r[k] = v
        return ins

    def barrier(self):
        for engine, e in self.eng.items():
            seen = self.seen[engine]
            for k, v in self.cnt.items():
                if v == 0 or seen.get(k, 0) >= v or engine == k:
                    continue
                seen[k] = v
                e.wait_ge(self.sem[k], v)

    def final_wait(self):
        e = self.eng["sp"]
        seen = self.seen["sp"]
        for k, v in self.cnt.items():
            if v == 0 or seen.get(k, 0) >= v:
                continue
            seen[k] = v
            e.wait_ge(self.sem[k], v)

    def dma(self, out, in_, reads=(), writes=(), engine="sp", **kw):
        return self.emit(engine, lambda e: e.dma_start(out=out, in_=in_, **kw), reads, writes, dma=True)

    def act(self, out, in_, func, reads=(), writes=(), **kw):
        return self.emit("act", lambda e: e.activation(out=out, in_=in_, func=func, **kw), reads, writes)

    def mm(self, out, lhsT, rhs, start, stop, reads=(), writes=(), inc=None, **kw):
        if inc is None:
            inc = stop
        return self.emit("pe", lambda e: e.matmul(out, lhsT, rhs, start=start, stop=stop, **kw), reads, writes, inc=inc)

    def tr(self, out, in_, ident, reads=(), writes=(), inc=True):
        return self.emit("pe", lambda e: e.transpose(out, in_, ident), reads, writes, inc=inc)

    def v(self, name, *args, reads=(), writes=(), engine="dve", **kw):
        return self.emit(engine, lambda e: getattr(e, name)(*args, **kw), reads, writes)


def lam_init_fn(layer):
    return 0.8 - 0.6 * math.exp(-0.3 * layer)


def build(cfg):
    NC, NB, SEQ, DEC_B, NPAGES, NPHYS, DEPTH = (cfg[k] for k in ("NC", "NB", "SEQ", "DEC_B", "NPAGES", "NPHYS", "DEPTH"))
    NPS = NB // NC
    NSS = DEC_B // NC
    NST = NSS * 4
    NE = (DEPTH + 1) // 2
    NO = DEPTH // 2
    NTP = NPS * SEQ
    NTILE = SEQ // TT
    NBLK = SEQ // 128
    assert NST <= 128 and SEQ % TT == 0

    nc = bass.Bass("TRN2", target_bir_lowering=False)

    def din(name, shape, dt=F32):
        return nc.dram_tensor(name, list(shape), dt, kind="ExternalInput")

    def dout(name, shape, dt=F32):
        return nc.dram_tensor(name, list(shape), dt, kind="ExternalOutput")

    def dscr(name, shape, dt=F32):
        return nc.dram_tensor(name, list(shape), dt, kind="Internal")

    xp = din("xp", [NTP, D]).ap()
    xs = din("xs", [NST, D]).ap()
    ck = din("ck", [NE * NPHYS * 128, 512]).ap()
    cv = din("cv", [NE * NPHYS * 128, 512]).ap()
    pt_h = din("pt", [1, NSS * NPAGES], I32)
    sconv = din("sconv", [NE, NSS, 30, 512]).ap()
    sre = din("sre", [max(NO, 1), NSS * 32, 128]).ap()
    sim = din("sim", [max(NO, 1), NSS * 32, 128]).ap()
    g_h = {k: din(k, [DEPTH, D]) for k in ("g_mix_pre", "g_mix_post", "g_ffn_pre", "g_ffn_post")}
    w_in_even = din("w_in_even", [NE, D, 2560]).ap()
    lqk_h = din("lambda_qk", [NE, 256])
    subln_h = din("subln_g", [NE, 128])
    conv_w = din("conv_w", [NE, 31, 512]).ap()
    conv_b = din("conv_b", [NE, 4, 128]).ap()
    ln_g = din("conv_ln_g", [NE, 4, 128]).ap()
    ln_b = din("conv_ln_b", [NE, 4, 128]).ap()
    w_out_even = din("w_out_even", [NE, D, D]).ap()
    w_in_odd = din("w_in_odd", [max(NO, 1), D, D]).ap()
    a_re_d = din("ssm_a_re", [max(NO, 1), 32, 128]).ap()
    a_im_d = din("ssm_a_im", [max(NO, 1), 32, 128]).ap()
    ldt_d = din("ssm_log_dt_rep", [max(NO, 1), 32, 128]).ap()
    b_re_d = din("ssm_b_re", [max(NO, 1), 64, 64, 16]).ap()
    b_im_d = din("ssm_b_im", [max(NO, 1), 64, 64, 16]).ap()
    c_re_d = din("ssm_c_re", [max(NO, 1), 64, 16, 64]).ap()
    c_im_d = din("ssm_c_im", [max(NO, 1), 64, 16, 64]).ap()
    ssm_d = din("ssm_d", [max(NO, 1), 8, 128]).ap()
    w_gate_odd = din("w_gate_odd", [max(NO, 1), D, D]).ap()
    w_out_odd = din("w_out_odd", [max(NO, 1), D, D]).ap()
    w_up = din("w_ffn_up", [DEPTH, D, DFF]).ap()
    w_dn = din("w_ffn_down", [DEPTH, DFF, D]).ap()
    consts = din("consts", [128, 512]).ap()

    y_p = dout("y_p", [NTP, D]).ap()
    y_s = dout("y_s", [NST, D]).ap()
    nk_p = dout("nk_p", [NE, NTP, 512]).ap()
    nv_p = dout("nv_p", [NE, NTP, 512]).ap()
    nk_s = dout("nk_s", [NE, NST, 512]).ap()
    nv_s = dout("nv_s", [NE, NST, 512]).ap()
    ncv_p = dout("ncv_p", [NE, NPS, 30, 512]).ap()
    ncv_s = dout("ncv_s", [NE, NSS, 30, 512]).ap()
    sr_p = dout("sr_p", [max(NO, 1), NPS * 32, 128]).ap()
    si_p = dout("si_p", [max(NO, 1), NPS * 32, 128]).ap()
    sr_s = dout("sr_s", [max(NO, 1), NSS * 32, 128]).ap()
    si_s = dout("si_s", [max(NO, 1), NSS * 32, 128]).ap()

    Hp = dscr("Hp", [NTP, D]).ap()
    Hs = dscr("Hs", [NST, D]).ap()
    QT = dscr("QT", [128, 4, SEQ], BF16).ap()
    CT = dscr("CT", [128, 4, SEQ], BF16).ap()
    DT = {"Hp": Buf(None), "Hs": Buf(None), "QT": Buf(None), "CT": Buf(None), "out": Buf(None)}

    def bcast(handle, off, n):
        return bass.AP(handle, off, [[0, 128], [1, n]])

    with ExitStack() as es0:
        p = Prog(nc, es0)

        uid = [0]

        def SB(es, name, shape, dt=F32):
            uid[0] += 1
            return Buf(es.enter_context(nc.sbuf_tensor("%s_%d" % (name, uid[0]), list(shape), dt)))

        def PSB(es, name, shape, dt=F32):
            return Buf(es.enter_context(nc.psum_tensor(name, list(shape), dt)), psum=True)

        PT = PSB(es0, "PT", [128, 1024], BF16)
        PA = [PSB(es0, "PA%d" % i, [128, 512]) for i in range(2)]
        PB = [PSB(es0, "PB%d" % i, [128, 512]) for i in range(2)]
        PO = [PSB(es0, "PO%d" % i, [128, 512]) for i in range(2)]
        PX = PSB(es0, "PX", [128, 512])

        cst = SB(es0, "cst", [128, 512])
        p.dma(cst[:], consts[:, :], writes=[cst])
        identf = cst[:, 0:128]
        iota_p = cst[:, 128:129]
        tri_f = cst[:, 129:257]
        mask4_f = cst[0:4, 257:289]
        cm0 = cst[0:8, 289:290]
        cm1 = cst[0:8, 290:291]
        sel_f = cst[0:8, 291:295]
        identb = SB(es0, "identb", [128, 128], BF16)
        p.v("tensor_copy", identb[:], identf, reads=[cst], writes=[identb])
        trib = SB(es0, "trib", [128, 128], BF16)
        p.v("tensor_copy", trib[:], tri_f, reads=[cst], writes=[trib])
        onesf = SB(es0, "onesf", [128, 128])
        p.v("memset", onesf[:], 1.0 / 512.0, writes=[onesf])
        small = SB(es0, "small", [128, 64])
        gidx = SB(es0, "gidx", [128, NE, NSS * NPAGES], I32)

        with ExitStack() as es:
            pti = SB(es, "pti", [128, NSS * NPAGES], I32)
            ptf = SB(es, "ptf", [128, NSS * NPAGES])
            p.dma(pti[:], bcast(pt_h, 0, NSS * NPAGES), writes=[pti])
            p.v("tensor_copy", ptf[:], pti[:], reads=[pti], writes=[ptf])
            p.v("tensor_scalar", ptf[:], ptf[:], 128.0, None, ALU.mult, reads=[ptf], writes=[ptf])
            p.v("tensor_scalar", ptf[:], ptf[:], iota_p, None, ALU.add, reads=[ptf, cst], writes=[ptf])
            for i in range(NE):
                if i > 0:
                    p.v("tensor_scalar", ptf[:], ptf[:], float(NPHYS * 128), None, ALU.add, reads=[ptf], writes=[ptf])
                p.v("tensor_copy", gidx[:, i, :], ptf[:], reads=[ptf], writes=[gidx])
            p.barrier()

        p.dma(Hp[:, :], xp[:, :], writes=[DT["Hp"]])
        p.dma(Hs[:, :], xs[:, :], writes=[DT["Hs"]])

        def rstd_calc(out_ap, in_ap, n_dim, eps, bufs):
            p.v("tensor_scalar", out_ap, in_ap, 1.0 / n_dim, float(eps), ALU.mult, ALU.add, reads=bufs, writes=bufs)
            p.act(out_ap, out_ap, AF.Ln, reads=bufs, writes=bufs)
            p.act(out_ap, out_ap, AF.Exp, reads=bufs, writes=bufs, scale=-0.5)

        def load_weight(W, src, nk, ncol):
            for k in range(nk):
                for c0 in range(0, ncol, 2048):
                    c1 = min(ncol, c0 + 2048)
                    p.dma(W[:, k, c0:c1], src[k * 128:(k + 1) * 128, c0:c1], writes=[W], engine="pool")

        class Tile:
            pass

        def make_tiles(tt=TT):
            tiles = []
            for sq in range(NPS):
                for ti in range(SEQ // tt):
                    t = Tile()
                    t.kind = "p"; t.seq = sq; t.ti = ti; t.ntok = tt; t.np = 128; t.nsub = tt // 128
                    t.row0 = sq * SEQ + ti * tt
                    t.first = ti == 0; t.last = ti == SEQ // tt - 1
                    tiles.append(t)
            t = Tile()
            t.kind = "s"; t.seq = 0; t.ti = 0; t.ntok = NST; t.np = NST; t.nsub = 1; t.row0 = 0
            t.first = True; t.last = True
            tiles.append(t)
            return tiles

        def h_src(t):
            H = Hp if t.kind == "p" else Hs
            if t.kind == "p":
                return H[t.row0:t.row0 + t.ntok, :].rearrange("(s q) d -> q s d", q=128), DT["Hp"]
            return H[0:NST, :].rearrange("(s q) d -> q s d", s=1), DT["Hs"]

        def load_h(t, hb):
            src, trk = h_src(t)
            p.dma(hb[0:t.np, 0:t.nsub, :], src, reads=[trk], writes=[hb])

        def store_h(t, hb, final):
            if final:
                Y = y_p if t.kind == "p" else y_s
                if t.kind == "p":
                    dst = Y[t.row0:t.row0 + t.ntok, :].rearrange("(s q) d -> q s d", q=128)
                else:
                    dst = Y[0:NST, :].rearrange("(s q) d -> q s d", s=1)
                p.dma(dst, hb[0:t.np, 0:t.nsub, :], reads=[hb], writes=[DT["out"]])
            else:
                dst, trk = h_src(t)
                p.dma(dst, hb[0:t.np, 0:t.nsub, :], reads=[hb], writes=[trk])

        def norm_to_xT(t, hb, gbc, xn, xT, ss):
            n = t.np
            for s in range(t.nsub):
                p.act(xn[0:n, s, :], hb[0:n, s, :], AF.Square, reads=[hb], writes=[xn, ss], accum_out=ss[0:n, s:s + 1])
            rstd_calc(ss[0:n, 0:t.nsub], ss[0:n, 0:t.nsub], D, 1e-6, [ss])
            for s in range(t.nsub):
                p.v("scalar_tensor_tensor", xn[0:n, s, :], hb[0:n, s, :], ss[0:n, s:s + 1], gbc[0:n, :], ALU.mult, ALU.mult,
                    reads=[hb, ss, gbc], writes=[xn])
            for s in range(t.nsub):
                for k in range(8):
                    p.tr(PT[:, k * 128:k * 128 + n], xn[0:n, s, k * 128:(k + 1) * 128], identb[0:n, 0:n],
                         reads=[xn, identb], writes=[PT], inc=(k == 7))
                p.v("tensor_copy", xT[:, :, s * 128:s * 128 + n], PT[:, :].rearrange("q (k c) -> q k c", k=8)[:, :, 0:n],
                    reads=[PT], writes=[xT])

        def post_norm_residual(t, s, hb, gbc, tb, ss2):
            n = t.np
            for half in range(2):
                p.act(tb[0:n, half * 512:(half + 1) * 512], PB[half][0:n, :], AF.Square, reads=[PB[half]], writes=[tb, ss2],
                      accum_out=ss2[0:n, half:half + 1])
            p.v("tensor_tensor", ss2[0:n, 2:3], ss2[0:n, 0:1], ss2[0:n, 1:2], ALU.add, reads=[ss2], writes=[ss2])
            rstd_calc(ss2[0:n, 2:3], ss2[0:n, 2:3], D, 1e-6, [ss2])
            for half in range(2):
                p.v("scalar_tensor_tensor", tb[0:n, half * 512:(half + 1) * 512], PB[half][0:n, :], ss2[0:n, 2:3],
                    gbc[0:n, half * 512:(half + 1) * 512], ALU.mult, ALU.mult, reads=[PB[half], ss2, gbc], writes=[tb])
            p.v("tensor_tensor", hb[0:n, s, :], hb[0:n, s, :], tb[0:n, :], ALU.add, reads=[hb, tb], writes=[hb], engine=os.environ.get("RESENG", "dve"))

        def out_proj(t, s, lhs_buf, lhs_fn, W, nk):
            n = t.np
            for half in range(2):
                for k in range(nk):
                    p.mm(PB[half][0:n, :], lhs_fn(k, s, n), W[:, k, half * 512:(half + 1) * 512], k == 0, k == nk - 1,
                         reads=[lhs_buf, W], writes=[PB[half]])

        def ffn_phase(l, final):
            with ExitStack() as es:
                Wup = SB(es, "Wup", [128, 8, DFF], BF16)
                Wdn = SB(es, "Wdn", [128, 32, D], BF16)
                gpre = SB(es, "gpre", [128, D]); gpost = SB(es, "gpost", [128, D])
                hbs = [SB(es, "hb%d" % i, [128, 2, D]) for i in range(2)]
                xn = SB(es, "xn", [128, 2, D], BF16)
                xT = SB(es, "xT", [128, 8, TT], BF16)
                hid = SB(es, "hid", [128, 32, TT], BF16)
                rb = [SB(es, "rb%d" % i, [128, TT], BF16) for i in range(2)]
                tb = SB(es, "tb", [128, D])
                ss = SB(es, "ss", [128, 4]); ss2 = SB(es, "ss2", [128, 4])
                p.dma(gpre[:], bcast(g_h["g_ffn_pre"], l * D, D), writes=[gpre])
                p.dma(gpost[:], bcast(g_h["g_ffn_post"], l * D, D), writes=[gpost])
                load_weight(Wup, w_up[l], 8, DFF)
                load_weight(Wdn, w_dn[l], 32, D)
                import os
                DBG = int(os.environ.get("DBGSTEP", "9"))
                for it, t in enumerate(make_tiles()):
                    hb = hbs[it % 2]
                    nt = t.ntok
                    if DBG < 1:
                        continue
                    load_h(t, hb)
                    if DBG < 2:
                        continue
                    norm_to_xT(t, hb, gpre, xn, xT, ss)
                    if DBG < 3:
                        continue
                    for m in range(32):
                        pa = PA[m % 2]
                        for k in range(8):
                            p.mm(pa[:, 0:nt], Wup[:, k, m * 128:(m + 1) * 128], xT[:, k, 0:nt], k == 0, k == 7,
                                 reads=[Wup, xT], writes=[pa])
                        r = rb[m % 2]
                        p.act(r[:, 0:nt], pa[:, 0:nt], AF.Relu, reads=[pa], writes=[r])
                        p.v("tensor_tensor", hid[:, m, 0:nt], r[:, 0:nt], r[:, 0:nt], ALU.mult, reads=[r], writes=[hid], engine=PENG2)
                    if DBG < 4:
                        continue
                    for s in range(t.nsub):
                        out_proj(t, s, hid, lambda k, s, n: hid[:, k, s * 128:s * 128 + n], Wdn, 32)
                        if DBG >= 5:
                            post_norm_residual(t, s, hb, gpost, tb, ss2)
                    store_h(t, hb, final)
                p.barrier()

        def odd_phase(l):
            i = l // 2
            TO = 128
            with ExitStack() as es:
                gpre = SB(es, "gpre", [128, D]); gpost = SB(es, "gpost", [128, D])
                LB = [SB(es, "LB%d" % r, [128, 32, 128], BF16) for r in range(2)]
                LC = [SB(es, "LC%d" % r, [128, 32, 128], BF16) for r in range(2)]
                U = [SB(es, "U%d" % r, [128, 32, TO]) for r in range(2)]
                st = SB(es, "st", [128, 20, 32])
                dsk = SB(es, "dsk", [128, 8])
                A_RE, A_IM, MAG, UL_RE, UL_IM, U1_RE, U1_IM, ULM_RE, ULM_IM = 0, 1, 2, 3, 4, 5, 6, 7, 8
                F_RE, F_IM, G_RE, G_IM, GL_RE, GL_IM, TMP0, TMP1, TMP2, TMP3 = 9, 10, 11, 12, 13, 14, 15, 16, 17, 18
                p.dma(gpre[:], bcast(g_h["g_mix_pre"], l * D, D), writes=[gpre])
                p.dma(gpost[:], bcast(g_h["g_mix_post"], l * D, D), writes=[gpost])
                p.dma(dsk[:], ssm_d[i].rearrange("j q -> q j"), writes=[dsk], allow_slow_non_contiguous=True)

                def S(k):
                    return st[:, k, :]

                def sop(name, *a, **kw):
                    p.v(name, *a, reads=[st], writes=[st], **kw)

                OS = int(os.environ.get("ODDSTEP", "9"))
                with ExitStack() as es2:
                    ld = SB(es2, "ld", [32, 3, 128])
                    p.dma(ld[:, 0, :], a_re_d[i], writes=[ld]); p.dma(ld[:, 1, :], a_im_d[i], writes=[ld]); p.dma(ld[:, 2, :], ldt_d[i], writes=[ld])
                    for r, dst in enumerate((TMP0, TMP1, TMP2)):
                        p.tr(PX[:, 0:32], ld[:, r, :], identf[0:32, 0:32], reads=[ld, cst], writes=[PX])
                        p.v("tensor_copy", S(dst), PX[:, 0:32], reads=[PX], writes=[st])
                    lr, li, dtt = S(TMP0), S(TMP1), S(TMP2)
                    p.act(dtt, dtt, AF.Exp, reads=[st], writes=[st])
                    sop("tensor_tensor", S(MAG), lr, dtt, ALU.mult)
                    p.act(S(MAG), S(MAG), AF.Exp, reads=[st], writes=[st])
                    sop("tensor_tensor", S(TMP3), li, dtt, ALU.mult)

                    def sincos(dst, shift):
                        x = S(G_RE); k = S(G_IM); ki = S(GL_RE)
                        sop("tensor_scalar", x, S(TMP3), float(shift), None, ALU.add)
                        sop("tensor_scalar", k, x, 1.0 / (2 * PI), None, ALU.mult)
                        p.v("tensor_copy", st[:, GL_RE, :].bitcast(I32), k, reads=[st], writes=[st])
                        p.v("tensor_copy", k, st[:, GL_RE, :].bitcast(I32), reads=[st], writes=[st])
                        sop("scalar_tensor_tensor", x, k, -2 * PI, x, ALU.mult, ALU.add)
                        sop("tensor_scalar", k, x, PI, -2 * PI, ALU.is_gt, ALU.mult)
                        sop("tensor_tensor", x, x, k, ALU.add)
                        sop("tensor_scalar", k, x, -PI, 2 * PI, ALU.is_lt, ALU.mult)
                        sop("tensor_tensor", x, x, k, ALU.add)
                        p.act(dst, x, AF.Sin, reads=[st], writes=[st])

                    sincos(S(U1_IM), 0.0)
                    sincos(S(U1_RE), PI / 2)
                    sop("tensor_tensor", S(A_RE), S(MAG), S(U1_RE), ALU.mult)
                    sop("tensor_tensor", S(A_IM), S(MAG), S(U1_IM), ALU.mult)
                    den = S(G_RE); nr = S(G_IM); t0_ = S(GL_RE); t1_ = S(GL_IM)
                    sop("tensor_tensor", den, lr, lr, ALU.mult)
                    sop("tensor_tensor", t0_, li, li, ALU.mult)
                    sop("tensor_tensor", den, den, t0_, ALU.add)
                    sop("reciprocal", den, den)
                    sop("tensor_scalar", nr, S(A_RE), -1.0, None, ALU.add)
                    sop("tensor_tensor", t0_, nr, lr, ALU.mult)
                    sop("tensor_tensor", t1_, S(A_IM), li, ALU.mult)
                    sop("tensor_tensor", t0_, t0_, t1_, ALU.add)
                    sop("tensor_tensor", S(F_RE), t0_, den, ALU.mult)
                    sop("tensor_tensor", t0_, S(A_IM), lr, ALU.mult)
                    sop("tensor_tensor", t1_, nr, li, ALU.mult)
                    sop("tensor_tensor", t0_, t0_, t1_, ALU.subtract)
                    sop("tensor_tensor", S(F_IM), t0_, den, ALU.mult)
                    p.v("memset", U[0][:, :, 0:1], 1.0, writes=[U[0]])
                    p.v("memset", U[1][:, :, 0:1], 0.0, writes=[U[1]])
                    sop("tensor_copy", S(UL_RE), S(U1_RE)); sop("tensor_copy", S(UL_IM), S(U1_IM))
                    n = 1 if OS >= 2 else TO
                    tmpa = SB(es2, "tmpa", [128, 32, TO // 2]); tmpb = SB(es2, "tmpb", [128, 32, TO // 2])
                    while n < TO:
                        cr = st[:, UL_RE, :].unsqueeze(2).to_broadcast([128, 32, n])
                        ci = st[:, UL_IM, :].unsqueeze(2).to_broadcast([128, 32, n])
                        a0 = U[0][:, :, 0:n]; a1 = U[1][:, :, 0:n]
                        p.v("tensor_tensor", tmpa[:, :, 0:n], a0, cr, ALU.mult, reads=[U[0], st], writes=[tmpa])
                        p.v("tensor_tensor", tmpb[:, :, 0:n], a1, ci, ALU.mult, reads=[U[1], st], writes=[tmpb])
                        p.v("tensor_tensor", U[0][:, :, n:2 * n], tmpa[:, :, 0:n], tmpb[:, :, 0:n], ALU.subtract, reads=[tmpa, tmpb], writes=[U[0]])
                        p.v("tensor_tensor", tmpa[:, :, 0:n], a0, ci, ALU.mult, reads=[U[0], st], writes=[tmpa])
                        p.v("tensor_tensor", tmpb[:, :, 0:n], a1, cr, ALU.mult, reads=[U[1], st], writes=[tmpb])
                        p.v("tensor_tensor", U[1][:, :, n:2 * n], tmpa[:, :, 0:n], tmpb[:, :, 0:n], ALU.add, reads=[tmpa, tmpb], writes=[U[1]])
                        sop("tensor_tensor", t0_, S(UL_RE), S(UL_RE), ALU.mult)
                        sop("tensor_tensor", t1_, S(UL_IM), S(UL_IM), ALU.mult)
                        sop("tensor_tensor", nr, S(UL_RE), S(UL_IM), ALU.mult)
                        sop("tensor_tensor", S(UL_RE), t0_, t1_, ALU.subtract)
                        sop("tensor_scalar", S(UL_IM), nr, 2.0, None, ALU.mult)
                        n *= 2
                    p.v("tensor_copy", S(ULM_RE), U[0][:, :, TO - 1], reads=[U[0], st], writes=[st])
                    p.v("tensor_copy", S(ULM_IM), U[1][:, :, TO - 1], reads=[U[1], st], writes=[st])
                    p.barrier()
                with ExitStack() as es2:
                  if OS >= 3:
                    Bs = [SB(es2, "Bs%d" % r, [128, 32, 128]) for r in range(2)]
                    Cs = [SB(es2, "Cs%d" % r, [128, 32, 128]) for r in range(2)]
                    for r in range(2):
                        p.v("memset", Bs[r][:], 0.0, writes=[Bs[r]], engine=PENG)
                        p.v("memset", Cs[r][:], 0.0, writes=[Cs[r]], engine=PENG)
                    for r, (bd, cd) in enumerate(((b_re_d, c_re_d), (b_im_d, c_im_d))):
                        for two in range(2):
                            for q4 in range(4):
                                bsrc = bd[i].rearrange("(m e) p c -> e p m c", e=8)[2 * q4 + two]
                                bdst = Bs[r][64 * two:64 * two + 64, :, :].rearrange("p (m f) c -> p m f c", f=4)[:, :, q4, 32 * q4 + 16 * two:32 * q4 + 16 * two + 16]
                                p.dma(bdst, bsrc, writes=[Bs[r]])
                                csrc = cd[i].rearrange("(m e) c p -> e c m p", e=8)[2 * q4 + two]
                                c0 = 32 * q4 + 16 * two
                                cdst = Cs[r][c0:c0 + 16, :, :].rearrange("c (m f) p -> c m f p", f=4)[:, :, q4, 64 * two:64 * two + 64]
                                p.dma(cdst, csrc, writes=[Cs[r]])
                    tbb = SB(es2, "tbb", [128, 128]); tbc = SB(es2, "tbc", [128, 128])
                    for s in range(32):
                        fr = st[:, F_RE, s:s + 1]; fi = st[:, F_IM, s:s + 1]
                        p.v("tensor_scalar", tbb[:], Bs[1][:, s, :], fi, None, ALU.mult, reads=[Bs[1], st], writes=[tbb])
                        p.v("scalar_tensor_tensor", tbc[:], Bs[0][:, s, :], fr, tbb[:], ALU.mult, ALU.subtract, reads=[Bs[0], st, tbb], writes=[tbc])
                        p.tr(PX[:, 0:128], tbc[:], identf, reads=[tbc, cst], writes=[PX])
                        p.act(LB[0][:, s, :], PX[:, 0:128], AF.Copy, reads=[PX], writes=[LB[0]])
                        p.v("tensor_scalar", tbb[:], Bs[0][:, s, :], fi, None, ALU.mult, reads=[Bs[0], st], writes=[tbb])
                        p.v("scalar_tensor_tensor", tbc[:], Bs[1][:, s, :], fr, tbb[:], ALU.mult, ALU.add, reads=[Bs[1], st, tbb], writes=[tbc])
                        p.tr(PX[:, 128:256], tbc[:], identf, reads=[tbc, cst], writes=[PX])
                        p.act(LB[1][:, s, :], PX[:, 128:256], AF.Copy, reads=[PX], writes=[LB[1]])
                        p.tr(PX[:, 256:384], Cs[0][:, s, :], identf, reads=[Cs[0], cst], writes=[PX])
                        p.act(LC[0][:, s, :], PX[:, 256:384], AF.Copy, reads=[PX], writes=[LC[0]])
                        p.tr(PX[:, 384:512], Cs[1][:, s, :], identf, reads=[Cs[1], cst], writes=[PX])
                        p.act(LC[1][:, s, :], PX[:, 384:512], AF.Copy, reads=[PX], writes=[LC[1]], scale=-1.0)
                    p.barrier()

                Win = SB(es, "Win", [128, 8, D], BF16)
                Wg = SB(es, "Wg", [128, 8, D], BF16)
                Wo = SB(es, "Wo", [128, 8, D], BF16)
                load_weight(Win, w_in_odd[i], 8, D)
                load_weight(Wg, w_gate_odd[i], 8, D)
                load_weight(Wo, w_out_odd[i], 8, D)
                hbs = [SB(es, "hb%d" % k, [128, 1, D]) for k in range(2)]
                xn = SB(es, "xn", [128, 1, D], BF16)
                xT = SB(es, "xT", [128, 8, TO], BF16)
                uTb = SB(es, "uTb", [128, 8, TO], BF16)
                uTf = SB(es, "uTf", [128, 8, TO])
                zT = SB(es, "zT", [128, 8, TO], BF16)
                zzT = SB(es, "zzT", [128, 8, TO], BF16)
                sgb = [SB(es, "sgb%d" % k, [128, TO], BF16) for k in range(2)]
                tb = SB(es, "tb", [128, D])
                ss = SB(es, "ss", [128, 4]); ss2 = SB(es, "ss2", [128, 4])
                w1 = SB(es, "w1", [128, TO]); w2 = SB(es, "w2", [128, TO])
                bpr = SB(es, "bpr", [128, TO]); bpi = SB(es, "bpi", [128, TO])
                gr = SB(es, "gr", [128, TO]); gi = SB(es, "gi", [128, TO])
                rr = SB(es, "rr", [128, TO])
                wt2 = [[SB(es, "wt%d_%d" % (q, k), [128, TO]) for k in range(8)] for q in range(2)]
                dbl = [(bpr, bpi, gr, gi, rr), tuple(SB(es, "dbl%d" % k, [128, TO]) for k in range(5))]
                hre = [SB(es, "hre%d" % k, [128, TO], BF16) for k in range(2)]
                him = [SB(es, "him%d" % k, [128, TO], BF16) for k in range(2)]
                BU = [SB(es, "BU%d" % r, [128, 32, NST]) for r in range(2)]
                HS = [SB(es, "HS%d" % r, [128, 32, NST], BF16) for r in range(2)]
                hcur = [SB(es, "hcur%d" % r, [128, 32, NSS]) for r in range(2)]
                hnew = [SB(es, "hnew%d" % r, [128, 32, NSS]) for r in range(2)]
                hq = SB(es, "hq", [128, 4, 32, NSS])
                hT = SB(es, "hT", [128, NSS * 32])
                stl = SB(es, "stl", [128, 2, 128])

                for it, t in enumerate(make_tiles(TO)):
                    hb = hbs[it % 2]
                    nt = t.ntok
                    if OS < 5:
                        continue
                    if os.environ.get("TILEK", t.kind) != t.kind:
                        continue
                    load_h(t, hb)
                    norm_to_xT(t, hb, gpre, xn, xT, ss)
                    for m in range(8):
                        pa = PA[m % 2]
                        for k in range(8):
                            p.mm(pa[:, 0:nt], Win[:, k, m * 128:(m + 1) * 128], xT[:, k, 0:nt], k == 0, k == 7, reads=[Win, xT], writes=[pa])
                        p.act(uTb[:, m, 0:nt], pa[:, 0:nt], AF.Copy, reads=[pa], writes=[uTb])
                        p.v("tensor_copy", uTf[:, m, 0:nt], pa[:, 0:nt], reads=[pa], writes=[uTf])
                    if OS < 6:
                        continue

                    def bu_mm(s, banks=None):
                        j = s // 4
                        b0, b1 = banks if banks is not None else (PA[0], PA[1])
                        p.mm(b0[:, 0:nt], LB[0][:, s, :], uTb[:, j, 0:nt], True, True, reads=[LB[0], uTb], writes=[b0])
                        p.mm(b1[:, 0:nt], LB[1][:, s, :], uTb[:, j, 0:nt], True, True, reads=[LB[1], uTb], writes=[b1])

                    def y_mm(s, hr_buf, hr_ap, hi_buf, hi_ap):
                        j = s // 4
                        pb = PB[j % 2]
                        p.mm(pb[:, 0:nt], LC[0][:, s, :], hr_ap, s % 4 == 0, False, reads=[LC[0], hr_buf], writes=[pb])
                        p.mm(pb[:, 0:nt], LC[1][:, s, :], hi_ap, False, s % 4 == 3, reads=[LC[1], hi_buf], writes=[pb])
                        if s % 4 == 3:
                            p.v("scalar_tensor_tensor", w1[:, 0:nt], uTf[:, j, 0:nt], dsk[:, j:j + 1], pb[:, 0:nt], ALU.mult, ALU.add,
                                reads=[uTf, dsk, pb], writes=[w1])
                            p.act(zT[:, j, 0:nt], w1[:, 0:nt], AF.Gelu, reads=[w1], writes=[zT])

                    if t.kind == "p":
                        if t.first:
                            p.v("memset", st[:, G_RE, :], 0.0, reads=[st], writes=[st])
                            p.v("memset", st[:, G_IM, :], 0.0, reads=[st], writes=[st])
                        for s in range(32):
                            pr_, pi_ = (PA[0], PA[1]) if s % 2 == 0 else (PO[0], PO[1])
                            bu_mm(s, (pr_, pi_))
                            ur = U[0][:, s, :]; ui = U[1][:, s, :]
                            bpr, bpi, gr, gi, rr = dbl[s % 2]
                            wt = wt2[s % 2]
                            wa, wb, wc, wd, we, wf, wg, wh = wt
                            p.v("tensor_tensor", wa[:], pr_[:, 0:TO], ur, ALU.mult, reads=[pr_, U[0]], writes=[wa])
                            p.v("tensor_tensor", wb[:], pi_[:, 0:TO], ui, ALU.mult, reads=[pi_, U[1]], writes=[wb])
                            p.v("tensor_tensor", bpr[:], wa[:], wb[:], ALU.add, reads=[wa, wb], writes=[bpr], engine=PENG2)
                            p.v("tensor_tensor", wc[:], pi_[:, 0:TO], ur, ALU.mult, reads=[pi_, U[0]], writes=[wc])
                            p.v("tensor_tensor", wd[:], pr_[:, 0:TO], ui, ALU.mult, reads=[pr_, U[1]], writes=[wd])
                            p.v("tensor_tensor", bpi[:], wc[:], wd[:], ALU.subtract, reads=[wc, wd], writes=[bpi], engine=PENG2)
                            p.act(rr[:], U[0][:, s, :], AF.Identity, reads=[U[0], st], writes=[rr], scale=0.0, bias=st[:, MAG, s:s + 1])
                            p.v("tensor_tensor_scan", gr[:], rr[:], bpr[:], st[:, G_RE, s:s + 1], ALU.mult, ALU.add, reads=[rr, bpr, st], writes=[gr])
                            p.v("tensor_tensor_scan", gi[:], rr[:], bpi[:], st[:, G_IM, s:s + 1], ALU.mult, ALU.add, reads=[rr, bpi, st], writes=[gi])
                            p.act(st[:, GL_RE, s:s + 1], gr[:, TO - 1:TO], AF.Copy, reads=[gr, st], writes=[st])
                            p.act(st[:, GL_IM, s:s + 1], gi[:, TO - 1:TO], AF.Copy, reads=[gi, st], writes=[st])
                            hr = hre[s % 2]; hi = him[s % 2]
                            p.v("tensor_tensor", we[:], gr[:], ur, ALU.mult, reads=[gr, U[0]], writes=[we])
                            p.v("tensor_tensor", wf[:], gi[:], ui, ALU.mult, reads=[gi, U[1]], writes=[wf])
                            p.v("tensor_tensor", hr[:], we[:], wf[:], ALU.subtract, reads=[we, wf], writes=[hr], engine=PENG2)
                            p.v("tensor_tensor", wg[:], gr[:], ui, ALU.mult, reads=[gr, U[1]], writes=[wg])
                            p.v("tensor_tensor", wh[:], gi[:], ur, ALU.mult, reads=[gi, U[0]], writes=[wh])
                            p.v("tensor_tensor", hi[:], wg[:], wh[:], ALU.add, reads=[wg, wh], writes=[hi], engine=PENG2)
                            y_mm(s, hr, hr[:], hi, hi[:])
                        if t.last:
                            sop("tensor_tensor", S(TMP0), S(GL_RE), S(ULM_RE), ALU.mult)
                            sop("tensor_tensor", S(TMP1), S(GL_IM), S(ULM_IM), ALU.mult)
                            sop("tensor_tensor", S(TMP2), S(TMP0), S(TMP1), ALU.subtract)
                            sop("tensor_tensor", S(TMP0), S(GL_RE), S(ULM_IM), ALU.mult)
                            sop("tensor_tensor", S(TMP1), S(GL_IM), S(ULM_RE), ALU.mult)
                            sop("tensor_tensor", S(TMP3), S(TMP0), S(TMP1), ALU.add)
                            for r, (src, dstd) in enumerate(((TMP2, sr_p), (TMP3, si_p))):
                                p.tr(PX[0:32, r * 128:(r + 1) * 128], st[:, src, :], identf, reads=[st, cst], writes=[PX])
                                p.act(stl[0:32, r, :], PX[0:32, r * 128:(r + 1) * 128], AF.Copy, reads=[PX], writes=[stl])
                                p.dma(dstd[i, t.seq * 32:(t.seq + 1) * 32, :], stl[0:32, r, :], reads=[stl], writes=[DT["out"]])
                        else:
                            sop("tensor_tensor", S(TMP0), S(GL_RE), S(UL_RE), ALU.mult)
                            sop("tensor_tensor", S(TMP1), S(GL_IM), S(UL_IM), ALU.mult)
                            sop("tensor_tensor", S(G_RE), S(TMP0), S(TMP1), ALU.subtract)
                            sop("tensor_tensor", S(TMP0), S(GL_RE), S(UL_IM), ALU.mult)
                            sop("tensor_tensor", S(TMP1), S(GL_IM), S(UL_RE), ALU.mult)
                            sop("tensor_tensor", S(G_IM), S(TMP0), S(TMP1), ALU.add)
                    else:
                        for s in range(32):
                            bu_mm(s)
                            p.act(BU[0][:, s, :], PA[0][:, 0:nt], AF.Copy, reads=[PA[0]], writes=[BU[0]])
                            p.v("tensor_copy", BU[1][:, s, :], PA[1][:, 0:nt], reads=[PA[1]], writes=[BU[1]])
                        for r, sd in enumerate((sre, sim)):
                            nrow = NSS * 32
                            for c0 in range(0, nrow, 128):
                                c1 = min(nrow, c0 + 128)
                                p.dma(stl[0:c1 - c0, r, :], sd[i, c0:c1, :], writes=[stl])
                                p.tr(PX[:, 0:c1 - c0], stl[0:c1 - c0, r, :], identf[0:c1 - c0, 0:c1 - c0], reads=[stl, cst], writes=[PX])
                                nb = (c1 - c0) // 32
                                p.v("tensor_copy", hcur[r][:, :, c0 // 32:c0 // 32 + nb],
                                    PX[:, 0:c1 - c0].rearrange("q (b s) -> q s b", s=32), reads=[PX], writes=[hcur[r]])
                        are = st[:, A_RE, :].unsqueeze(2).to_broadcast([128, 32, NSS])
                        aim = st[:, A_IM, :].unsqueeze(2).to_broadcast([128, 32, NSS])
                        for tt_ in range(4):
                            bur = BU[0][:, :, :].rearrange("q s (b t) -> q s b t", t=4)[:, :, :, tt_]
                            bui = BU[1][:, :, :].rearrange("q s (b t) -> q s b t", t=4)[:, :, :, tt_]
                            p.v("tensor_tensor", hq[:, 0], hcur[0][:], are, ALU.mult, reads=[hcur[0], st], writes=[hq])
                            p.v("tensor_tensor", hq[:, 1], hcur[1][:], aim, ALU.mult, reads=[hcur[1], st], writes=[hq])
                            p.v("tensor_tensor", hq[:, 2], hcur[0][:], aim, ALU.mult, reads=[hcur[0], st], writes=[hq])
                            p.v("tensor_tensor", hq[:, 3], hcur[1][:], are, ALU.mult, reads=[hcur[1], st], writes=[hq])
                            p.v("tensor_tensor", hq[:, 0], hq[:, 0], hq[:, 1], ALU.subtract, reads=[hq], writes=[hq])
                            p.v("tensor_tensor", hq[:, 2], hq[:, 2], hq[:, 3], ALU.add, reads=[hq], writes=[hq])
                            p.v("tensor_tensor", hnew[0][:], hq[:, 0], bur, ALU.add, reads=[hq, BU[0]], writes=[hnew[0]])
                            p.v("tensor_tensor", hnew[1][:], hq[:, 2], bui, ALU.add, reads=[hq, BU[1]], writes=[hnew[1]])
                            for r in range(2):
                                p.v("tensor_copy", HS[r][:, :, :].rearrange("q s (b t) -> q s b t", t=4)[:, :, :, tt_], hnew[r][:],
                                    reads=[hnew[r]], writes=[HS[r]], engine=PENG)
                                p.v("tensor_copy", hcur[r][:], hnew[r][:], reads=[hnew[r]], writes=[hcur[r]])
                        for r, dstd in enumerate((sr_s, si_s)):
                            nrow = NSS * 32
                            p.v("tensor_copy", hT[:, :].rearrange("q (b s) -> q b s", s=32), hcur[r][:, :, :].rearrange("q s b -> q b s"),
                                reads=[hcur[r]], writes=[hT])
                            for c0 in range(0, nrow, 128):
                                c1 = min(nrow, c0 + 128)
                                p.tr(PX[0:c1 - c0, 0:128], hT[:, c0:c1], identf, reads=[hT, cst], writes=[PX])
                                p.act(stl[0:c1 - c0, r, :], PX[0:c1 - c0, 0:128], AF.Copy, reads=[PX], writes=[stl])
                                p.dma(dstd[i, c0:c1, :], stl[0:c1 - c0, r, :], reads=[stl], writes=[DT["out"]])
                        for s in range(32):
                            y_mm(s, HS[0], HS[0][:, s, :], HS[1], HS[1][:, s, :])
                    for m in range(8):
                        pa = PA[m % 2]
                        for k in range(8):
                            p.mm(pa[:, 0:nt], Wg[:, k, m * 128:(m + 1) * 128], zT[:, k, 0:nt], k == 0, k == 7, reads=[Wg, zT], writes=[pa])
                        sg = sgb[m % 2]
                        p.act(sg[:, 0:nt], pa[:, 0:nt], AF.Sigmoid, reads=[pa], writes=[sg])
                        p.v("tensor_tensor", zzT[:, m, 0:nt], zT[:, m, 0:nt], sg[:, 0:nt], ALU.mult, reads=[zT, sg], writes=[zzT], engine=PENG2)
                    for s in range(t.nsub):
                        out_proj(t, s, zzT, lambda k, s, n: zzT[:, k, s * 128:s * 128 + n], Wo, 8)
                        post_norm_residual(t, s, hb, gpost, tb, ss2)
                    store_h(t, hb, False)
                p.barrier()

        def even_phase(l):
            i = l // 2
            lam_init = lam_init_fn(l)
            with ExitStack() as es:
                gpre = SB(es, "gpre", [128, D]); gpost = SB(es, "gpost", [128, D])
                KT = SB(es, "KT", [128, 4, SEQ], BF16)
                VE = SB(es, "VE", [128, NBLK, 4, 129], BF16)
                KTs = SB(es, "KTs", [128, 4, NST], BF16)
                qTs = SB(es, "qTs", [128, 4, NST], BF16)
                cTs = SB(es, "cTs", [128, 4, NST], BF16)
                VEs = SB(es, "VEs", [4, NSS, 4, 129], BF16)
                lam = SB(es, "lam", [128, 8])
                sgbc = SB(es, "sgbc", [128, 128])
                cw = SB(es, "cw", [128, 4, 31]); cbias = SB(es, "cbias", [128, 4]); lng = SB(es, "lng", [128, 4]); lnb = SB(es, "lnb", [128, 4])
                p.dma(gpre[:], bcast(g_h["g_mix_pre"], l * D, D), writes=[gpre])
                p.dma(gpost[:], bcast(g_h["g_mix_post"], l * D, D), writes=[gpost])
                p.v("memset", VE[:, :, :, 128:129], 1.0, writes=[VE])
                p.v("memset", VEs[:, :, :, 128:129], 1.0, writes=[VEs])
                with ExitStack() as es2:
                    lq = SB(es2, "lq", [128, 256]); lt = SB(es2, "lt", [128, 128])
                    p.dma(lq[:], bcast(lqk_h, i * 256, 256), writes=[lq])
                    p.v("tensor_tensor", lt[:, 0:64], lq[:, 0:64], lq[:, 64:128], ALU.mult, reads=[lq], writes=[lt])
                    p.v("tensor_tensor", lt[:, 64:128], lq[:, 128:192], lq[:, 192:256], ALU.mult, reads=[lq], writes=[lt])
                    p.v("tensor_reduce", lam[:, 0:2], lt[:, :].rearrange("q (a b) -> q a b", a=2), AX.X, ALU.add, reads=[lt], writes=[lam])
                    p.act(lam[:, 0:2], lam[:, 0:2], AF.Exp, reads=[lam], writes=[lam])
                    p.v("tensor_tensor", lam[:, 2:3], lam[:, 0:1], lam[:, 1:2], ALU.subtract, reads=[lam], writes=[lam])
                    p.v("tensor_scalar", lam[:, 3:4], lam[:, 2:3], float(lam_init), -1.0, ALU.add, ALU.mult, reads=[lam], writes=[lam])
                    p.v("scalar_tensor_tensor", lam[0:8, 4:5], cm1, lam[0:8, 3:4], cm0, ALU.mult, ALU.add, reads=[cst, lam], writes=[lam])
                    p.dma(sgbc[:], bcast(subln_h, i * 128, 128), writes=[sgbc])
                    p.v("tensor_scalar", sgbc[:], sgbc[:], float(1.0 - lam_init), None, ALU.mult, reads=[sgbc], writes=[sgbc])
                    cwl = SB(es2, "cwl", [31, 512])
                    p.dma(cwl[:], conv_w[i], writes=[cwl])
                    for m in range(4):
                        p.tr(PX[:, m * 32:m * 32 + 31], cwl[:, m * 128:(m + 1) * 128], identf[0:31, 0:31], reads=[cwl, cst], writes=[PX])
                    p.v("tensor_copy", cw[:], PX[:, 0:128].rearrange("q (m j) -> q m j", m=4)[:, :, 0:31], reads=[PX], writes=[cw])
                    p.dma(cbias[:], conv_b[i].rearrange("m q -> q m"), writes=[cbias], allow_slow_non_contiguous=True)
                    p.dma(lng[:], ln_g[i].rearrange("m q -> q m"), writes=[lng], allow_slow_non_contiguous=True)
                    p.dma(lnb[:], ln_b[i].rearrange("m q -> q m"), writes=[lnb], allow_slow_non_contiguous=True)
                    p.barrier()
                neglam = lam[:, 3:4]

                tiles = make_tiles()
                ptiles = [t for t in tiles if t.kind == "p"]
                stile = tiles[-1]
                for sq in range(NPS):
                    seq_tiles = [t for t in ptiles if t.seq == sq]
                    with ExitStack() as esA:
                        phaseA(l, i, esA, seq_tiles + ([stile] if sq == NPS - 1 else []), gpre, KT, VE, KTs, qTs, cTs, VEs, cw, cbias, lng, lnb)
                    with ExitStack() as esB:
                        phaseB(l, i, esB, seq_tiles, stile if sq == NPS - 1 else None, gpost, KT, VE, KTs, qTs, cTs, VEs, lam, neglam, sgbc)
                p.barrier()

        def phaseA(l, i, es, tiles, gpre, KT, VE, KTs, qTs, cTs, VEs, cw, cbias, lng, lnb):
            Win = SB(es, "WinE", [128, 8, 2560], BF16)
            load_weight(Win, w_in_even[i], 8, 2560)
            hbs = [SB(es, "hbA%d" % k, [128, 2, D]) for k in range(2)]
            xn = SB(es, "xnA", [128, 2, D], BF16)
            xT = SB(es, "xTA", [128, 8, TT], BF16)
            qst = SB(es, "qst", [128, 4, TT], BF16)
            cst_ = SB(es, "cstA", [128, 4, TT], BF16)
            kvst = [SB(es, "kvst%d" % k, [128, 512]) for k in range(2)]
            ss = SB(es, "ssA", [128, 4])
            Ub = SB(es, "Ub", [128, 4, 30 + TT])
            FULL = SB(es, "FULL", [128, 4, NSS, 34])
            acc = SB(es, "acc", [128, 4, TT])
            sq_ = SB(es, "sqA", [128, 4, TT])
            sgf = SB(es, "sgf", [128, TT])
            mu = SB(es, "mu", [128, TT]); rs = SB(es, "rs", [128, TT]); w3 = SB(es, "w3", [128, TT])
            utm = SB(es, "utm", [128, 512]); sgt = SB(es, "sgt", [128, 512])
            scl = SB(es, "scl", [120, 512])
            vbf = SB(es, "vbf", [NST, 512], BF16)
            for it, t in enumerate(tiles):
                hb = hbs[it % 2]
                nt = t.ntok; n = t.np
                isp = t.kind == "p"
                load_h(t, hb)
                norm_to_xT(t, hb, gpre, xn, xT, ss)
                for hh in range(4):
                    pa = PA[hh % 2]
                    for k in range(8):
                        p.mm(pa[:, 0:nt], Win[:, k, hh * 128:(hh + 1) * 128], xT[:, k, 0:nt], k == 0, k == 7, reads=[Win, xT], writes=[pa])
                    if isp:
                        p.act(qst[:, hh, 0:nt], pa[:, 0:nt], AF.Copy, reads=[pa], writes=[qst])
                    else:
                        p.act(qTs[:, hh, :], pa[:, 0:nt], AF.Copy, reads=[pa], writes=[qTs])
                for hh in range(4):
                    pa = PA[hh % 2]
                    for k in range(8):
                        p.mm(pa[:, 0:nt], Win[:, k, 512 + hh * 128:512 + (hh + 1) * 128], xT[:, k, 0:nt], k == 0, k == 7, reads=[Win, xT], writes=[pa])
                    if isp:
                        p.act(KT[:, hh, t.ti * TT:t.ti * TT + nt], pa[:, 0:nt], AF.Copy, reads=[pa], writes=[KT])
                    else:
                        p.act(KTs[:, hh, :], pa[:, 0:nt], AF.Copy, reads=[pa], writes=[KTs])
                if isp:
                    p.dma(QT[:, :, t.ti * TT:t.ti * TT + nt], qst[:, :, 0:nt], reads=[qst], writes=[DT["QT"]])
                for s in range(t.nsub):
                    for which, c0 in ((0, 512), (1, 1024)):
                        pb = PB[which]
                        for k in range(8):
                            p.mm(pb[0:n, :], xT[:, k, s * 128:s * 128 + n], Win[:, k, c0:c0 + 512], k == 0, k == 7, reads=[xT, Win], writes=[pb])
                        st_ = kvst[which]
                        p.act(st_[0:n, :], pb[0:n, :], AF.Copy, reads=[pb], writes=[st_])
                        if isp:
                            dst = (nk_p, nv_p)[which][i, t.row0 + s * 128:t.row0 + s * 128 + n, :]
                        else:
                            dst = (nk_s, nv_s)[which][i, 0:n, :]
                        p.dma(dst, st_[0:n, :], reads=[st_], writes=[DT["out"]])
                        if which == 1:
                            if isp:
                                blk = t.ti * (TT // 128) + s
                                p.v("tensor_copy", VE[:, blk, :, 0:128], pb[:, :].rearrange("q (h e) -> q h e", h=4), reads=[pb], writes=[VE])
                            else:
                                p.v("tensor_copy", vbf[:, :], pb[0:n, :], reads=[pb], writes=[vbf])
                                for b in range(NSS):
                                    p.dma(VEs[0:4, b, :, 0:128], vbf[4 * b:4 * b + 4, :].rearrange("q (h e) -> q h e", h=4), reads=[vbf], writes=[VEs])
                if isp and t.first:
                    p.v("memset", Ub[:, :, 0:30], 0.0, writes=[Ub])
                if not isp:
                    for b0 in range(0, NSS, 4):
                        nb = min(4, NSS - b0)
                        p.dma(scl[0:nb * 30, :], sconv[i, b0:b0 + nb].rearrange("b r c -> (b r) c"), writes=[scl])
                        for m in range(4):
                            p.tr(PX[:, m * 128:m * 128 + nb * 30], scl[0:nb * 30, m * 128:(m + 1) * 128], identf[0:nb * 30, 0:nb * 30],
                                 reads=[scl, cst], writes=[PX])
                        p.v("tensor_copy", FULL[:, :, b0:b0 + nb, 0:30],
                            PX[:, :].rearrange("q (m x) -> q m x", m=4)[:, :, 0:nb * 30].rearrange("q m (b r) -> q m b r", r=30), reads=[PX], writes=[FULL])
                        for b in range(b0, b0 + nb):
                            p.dma(ncv_s[i, b, 0:26, :], sconv[i, b, 4:30, :], writes=[DT["out"]])
                for m in range(4):
                    for k in range(8):
                        p.mm(PA[0][:, 0:nt], Win[:, k, 1536 + m * 128:1536 + (m + 1) * 128], xT[:, k, 0:nt], k == 0, k == 7, reads=[Win, xT], writes=[PA[0]])
                    for k in range(8):
                        p.mm(PA[1][:, 0:nt], Win[:, k, 2048 + m * 128:2048 + (m + 1) * 128], xT[:, k, 0:nt], k == 0, k == 7, reads=[Win, xT], writes=[PA[1]])
                    p.act(sgf[:, 0:nt], PA[1][:, 0:nt], AF.Sigmoid, reads=[PA[1]], writes=[sgf])
                    if isp:
                        p.v("tensor_tensor", Ub[:, m, 30:30 + nt], PA[0][:, 0:nt], sgf[:, 0:nt], ALU.mult, reads=[PA[0], sgf], writes=[Ub])
                    else:
                        p.v("tensor_tensor", FULL[:, m, :, 30:34], PA[0][:, 0:nt].rearrange("q (b t) -> q b t", t=4),
                            sgf[:, 0:nt].rearrange("q (b t) -> q b t", t=4), ALU.mult, reads=[PA[0], sgf], writes=[FULL])
                for m in range(4):
                    if isp:
                        src = lambda j: Ub[:, m, j:j + nt]
                        srcb = Ub
                        av = acc[:, m, 0:nt]
                    else:
                        src = lambda j: FULL[:, m, :, j:j + 4]
                        srcb = FULL
                        av = acc[:, m, 0:nt].rearrange("q (b t) -> q b t", t=4)
                    p.v("tensor_scalar", av, src(0), cw[:, m, 0:1], cbias[:, m:m + 1], ALU.mult, ALU.add, reads=[srcb, cw, cbias], writes=[acc])
                    for j in range(1, 31):
                        p.v("scalar_tensor_tensor", av, src(j), cw[:, m, j:j + 1], av, ALU.mult, ALU.add, reads=[srcb, cw, acc], writes=[acc])
                    p.act(sq_[:, m, 0:nt], acc[:, m, 0:nt], AF.Square, reads=[acc], writes=[sq_])
                if isp and not t.last:
                    p.v("tensor_copy", Ub[:, :, 0:30], Ub[:, :, nt:nt + 30], reads=[Ub], writes=[Ub], engine=PENG)
                for m in range(4):
                    p.mm(PA[0][:, 0:nt], onesf[:, :], acc[:, m, 0:nt], m == 0, m == 3, reads=[onesf, acc], writes=[PA[0]])
                for m in range(4):
                    p.mm(PA[1][:, 0:nt], onesf[:, :], sq_[:, m, 0:nt], m == 0, m == 3, reads=[onesf, sq_], writes=[PA[1]])
                p.v("tensor_copy", mu[:, 0:nt], PA[0][:, 0:nt], reads=[PA[0]], writes=[mu])
                p.v("tensor_tensor", w3[:, 0:nt], mu[:, 0:nt], mu[:, 0:nt], ALU.mult, reads=[mu], writes=[w3])
                p.v("tensor_tensor", rs[:, 0:nt], PA[1][:, 0:nt], w3[:, 0:nt], ALU.subtract, reads=[PA[1], w3], writes=[rs])
                rstd_calc(rs[:, 0:nt], rs[:, 0:nt], 1.0, 1e-5, [rs])
                for m in range(4):
                    p.v("tensor_tensor", w3[:, 0:nt], acc[:, m, 0:nt], mu[:, 0:nt], ALU.subtract, reads=[acc, mu], writes=[w3])
                    p.v("tensor_tensor", w3[:, 0:nt], w3[:, 0:nt], rs[:, 0:nt], ALU.mult, reads=[w3, rs], writes=[w3])
                    dstc = cst_[:, m, 0:nt] if isp else cTs[:, m, :]
                    p.act(dstc, w3[:, 0:nt], AF.Silu, reads=[w3, lng, lnb], writes=[cst_ if isp else cTs], scale=lng[:, m:m + 1], bias=lnb[:, m:m + 1])
                if isp:
                    p.dma(CT[:, :, t.ti * TT:t.ti * TT + nt], cst_[:, :, 0:nt], reads=[cst_], writes=[DT["CT"]])
                if t.last:
                    s = t.nsub - 1
                    for which, c0 in ((0, 1536), (1, 2048)):
                        for k in range(8):
                            p.mm(PB[which][0:n, :], xT[:, k, s * 128:s * 128 + n], Win[:, k, c0:c0 + 512], k == 0, k == 7, reads=[xT, Win], writes=[PB[which]])
                    p.act(sgt[0:n, :], PB[1][0:n, :], AF.Sigmoid, reads=[PB[1]], writes=[sgt])
                    p.v("tensor_tensor", utm[0:n, :], PB[0][0:n, :], sgt[0:n, :], ALU.mult, reads=[PB[0], sgt], writes=[utm])
                    if isp:
                        p.dma(ncv_p[i, t.seq, :, :], utm[98:128, :], reads=[utm], writes=[DT["out"]])
                    else:
                        for b in range(NSS):
                            p.dma(ncv_s[i, b, 26:30, :], utm[4 * b:4 * b + 4, :], reads=[utm], writes=[DT["out"]])
            p.barrier()

        def phaseB(l, i, es, ptiles, stile, gpost, KT, VE, KTs, qTs, cTs, VEs, lam, neglam, sgbc):
            Wo = SB(es, "WoE", [128, 8, D], BF16)
            load_weight(Wo, w_out_even[i], 8, D)
            hbs = [SB(es, "hbB%d" % k, [128, 2, D]) for k in range(2)]
            qTt = [SB(es, "qTt%d" % k, [128, 4, TT], BF16) for k in range(2)]
            mixT = [SB(es, "mixT%d" % k, [128, 8, TT], BF16) for k in range(2)]
            Pb = [SB(es, "Pb%d" % k, [128, TT], BF16) for k in range(2)]
            tb = SB(es, "tbB", [128, D])
            ss2 = SB(es, "ss2B", [128, 4])
            at = SB(es, "at", [128, 128]); at2 = SB(es, "at2", [128, 128]); an = SB(es, "an", [128, 128], BF16)
            sm = SB(es, "smB", [128, 16])
            for it, t in enumerate(ptiles):
                hb = hbs[it % 2]; qt_ = qTt[it % 2]; mx = mixT[it % 2]
                t0 = t.ti * TT
                load_h(t, hb)
                p.dma(qt_[:], QT[:, :, t0:t0 + TT], reads=[DT["QT"]], writes=[qt_])
                p.dma(mx[:, 4:8, :], CT[:, :, t0:t0 + TT], reads=[DT["CT"]], writes=[mx])
                nsub = TT // 128
                kb0 = t.ti * nsub
                nkb = kb0 + nsub
                for hh in range(4):
                    for c in range(2):
                        for kb in range(nkb):
                            j = kb - kb0
                            q0 = 128 * j if j > 0 else 0
                            nn = TT - q0
                            ps = PA[kb % 2]; pbuf = Pb[kb % 2]
                            p.mm(ps[:, 0:nn], KT[64 * c:64 * c + 64, hh, kb * 128:(kb + 1) * 128], qt_[64 * c:64 * c + 64, hh, q0:TT], True, True,
                                 reads=[KT, qt_], writes=[ps])
                            p.act(pbuf[:, 0:nn], ps[:, 0:nn], AF.Exp, reads=[ps], writes=[pbuf], scale=0.125)
                            if j >= 0:
                                p.v("tensor_tensor", pbuf[:, 0:128], pbuf[:, 0:128], trib[:, :], ALU.mult, reads=[pbuf, trib], writes=[pbuf], engine=PENG)
                            for sub in range(max(j, 0), nsub):
                                off = sub * 128 - q0
                                po = PO[c]
                                p.mm(po[:, sub * 129:sub * 129 + 129], pbuf[:, off:off + 128], VE[:, kb, hh, :], kb == 0 and sub == 0, kb == kb0 + sub,
                                     reads=[pbuf, VE], writes=[po], skip_group_check=True)
                    for sub in range(nsub):
                        o1 = PO[0][:, sub * 129:sub * 129 + 128]; l1 = PO[0][:, sub * 129 + 128:sub * 129 + 129]
                        o2 = PO[1][:, sub * 129:sub * 129 + 128]; l2 = PO[1][:, sub * 129 + 128:sub * 129 + 129]
                        p.v("reciprocal", sm[:, 0:1], l1, reads=[PO[0]], writes=[sm])
                        p.v("reciprocal", sm[:, 1:2], l2, reads=[PO[1]], writes=[sm])
                        p.v("tensor_tensor", sm[:, 1:2], sm[:, 1:2], neglam, ALU.mult, reads=[sm, lam], writes=[sm])
                        p.v("tensor_scalar", at[:], o1, sm[:, 0:1], None, ALU.mult, reads=[PO[0], sm], writes=[at])
                        p.v("scalar_tensor_tensor", at2[:], o2, sm[:, 1:2], at[:], ALU.mult, ALU.add, reads=[PO[1], sm, at], writes=[at2])
                        p.act(at[:], at2[:], AF.Square, reads=[at2], writes=[at, sm], accum_out=sm[:, 2:3])
                        rstd_calc(sm[:, 2:3], sm[:, 2:3], 128, 1e-5, [sm])
                        p.v("scalar_tensor_tensor", an[:], at2[:], sm[:, 2:3], sgbc[:], ALU.mult, ALU.mult, reads=[at2, sm, sgbc], writes=[an])
                        p.tr(PT[:, 0:128], an[:], identb[:], reads=[an, identb], writes=[PT])
                        p.v("tensor_copy", mx[:, hh, sub * 128:(sub + 1) * 128], PT[:, 0:128], reads=[PT], writes=[mx])
                for s in range(nsub):
                    out_proj(t, s, mx, lambda k, s, n: mx[:, k, s * 128:s * 128 + n], Wo, 8)
                    post_norm_residual(t, s, hb, gpost, tb, ss2)
                store_h(t, hb, False)
            if stile is not None:
                t = stile
                hb = hbs[0]; mx = mixT[0]
                load_h(t, hb)
                p.v("tensor_copy", mx[:, 4:8, 0:NST], cTs[:, :, :], reads=[cTs], writes=[mx], engine=PENG)
                Qb = SB(es, "Qb", [128, 4, 8], BF16)
                Kpg = [SB(es, "Kpg%d" % k, [128, 512], BF16) for k in range(3)]
                Vpg = [SB(es, "Vpg%d" % k, [128, 4, 129], BF16) for k in range(3)]
                Vraw = [SB(es, "Vraw%d" % k, [128, 512], BF16) for k in range(3)]
                KTp = [SB(es, "KTp%d" % k, [128, 512], BF16) for k in range(2)]
                Pp = [SB(es, "Pp%d" % k, [128, 32], BF16) for k in range(2)]
                Osb = SB(es, "Osb", [8, 4, 129])
                Rr = SB(es, "Rr", [8, 8]); SelR = SB(es, "SelR", [8, 4, 4])
                ats = SB(es, "ats", [4, 512]); ats2 = SB(es, "ats2", [4, 512]); ans = SB(es, "ans", [4, 512], BF16)
                sms = SB(es, "sms", [4, 8])
                m4b = SB(es, "m4b", [4, 32], BF16)
                p.v("tensor_copy", m4b[:], mask4_f, reads=[cst], writes=[m4b])
                for k in range(3):
                    p.v("memset", Vpg[k][:, :, 128:129], 1.0, writes=[Vpg[k]])
                p.v("memset", Qb[:], 0.0, writes=[Qb])
                pg = 0
                for b in range(NSS):
                    p.v("tensor_copy", Qb[0:64, :, 0:4], qTs[0:64, :, 4 * b:4 * b + 4], reads=[qTs], writes=[Qb])
                    p.v("tensor_copy", Qb[64:128, :, 4:8], qTs[64:128, :, 4 * b:4 * b + 4], reads=[qTs], writes=[Qb])
                    for n_ in range(NPAGES + 1):
                        newblk = n_ == NPAGES
                        pps = PX; ppb = Pp[n_ % 2]
                        if not newblk:
                            kp = Kpg[pg % 3]; vp = Vpg[pg % 3]; ktp = KTp[pg % 2]
                            pg += 1
                            col = b * NPAGES + n_
                            p.emit("pool", lambda e, kp=kp, col=col: e.indirect_dma_start(
                                out=kp[:, :], out_offset=None, in_=ck[:, :],
                                in_offset=bass.IndirectOffsetOnAxis(ap=gidx[:, i, col:col + 1], axis=0)), reads=[gidx], writes=[kp], dma=True)
                            vr = Vraw[(pg - 1) % 3]
                            p.emit("pool", lambda e, vr=vr, col=col: e.indirect_dma_start(
                                out=vr[:, :], out_offset=None, in_=cv[:, :],
                                in_offset=bass.IndirectOffsetOnAxis(ap=gidx[:, i, col:col + 1], axis=0)), reads=[gidx], writes=[vr], dma=True)
                            p.v("tensor_copy", vp[:, :, 0:128], vr[:, :].rearrange("q (h e) -> q h e", h=4), reads=[vr], writes=[vp])
                            for hh in range(4):
                                p.tr(PT[:, hh * 128:(hh + 1) * 128], kp[:, hh * 128:(hh + 1) * 128], identb[:], reads=[kp, identb], writes=[PT], inc=(hh == 3))
                            p.v("tensor_copy", ktp[:, :], PT[:, 0:512], reads=[PT], writes=[ktp])
                            for hh in range(4):
                                p.mm(pps[:, hh * 8:hh * 8 + 8], ktp[:, hh * 128:(hh + 1) * 128], Qb[:, hh, :], True, True, reads=[ktp, Qb], writes=[pps], inc=(hh == 3))
                            p.act(ppb[:, :], pps[:, 0:32], AF.Exp, reads=[pps], writes=[ppb], scale=0.125)
                            for hh in range(4):
                                po = PO[hh // 2]
                                p.mm(po[0:8, (hh % 2) * 129:(hh % 2) * 129 + 129], ppb[:, hh * 8:hh * 8 + 8], vp[:, hh, :], n_ == 0 and hh % 2 == 0, False,
                                     reads=[ppb, vp], writes=[po], inc=False, skip_group_check=True)
                        else:
                            for hh in range(4):
                                p.mm(pps[0:4, hh * 8:hh * 8 + 8], KTs[:, hh, 4 * b:4 * b + 4], Qb[:, hh, :], True, True, reads=[KTs, Qb], writes=[pps], inc=(hh == 3))
                            p.act(ppb[0:4, :], pps[0:4, 0:32], AF.Exp, reads=[pps], writes=[ppb], scale=0.125)
                            p.v("tensor_tensor", ppb[0:4, :], ppb[0:4, :], m4b[:, :], ALU.mult, reads=[ppb, m4b], writes=[ppb])
                            for hh in range(4):
                                po = PO[hh // 2]
                                p.mm(po[0:8, (hh % 2) * 129:(hh % 2) * 129 + 129], ppb[0:4, hh * 8:hh * 8 + 8], VEs[0:4, b, hh, :], False, True,
                                     reads=[ppb, VEs], writes=[po], inc=True, skip_group_check=True)
                    for hp in range(2):
                        p.act(Osb[:, 2 * hp:2 * hp + 2, :], PO[hp][0:8, 0:258].rearrange("q (h e) -> q h e", h=2), AF.Copy, reads=[PO[hp]], writes=[Osb])
                    p.v("reciprocal", Rr[:, 0:4], Osb[:, :, 128], reads=[Osb], writes=[Rr])
                    p.v("tensor_scalar", Rr[:, 0:4], Rr[:, 0:4], lam[0:8, 4:5], None, ALU.mult, reads=[Rr, lam], writes=[Rr])
                    for hh in range(4):
                        p.v("tensor_scalar", SelR[:, hh, :], sel_f, Rr[:, hh:hh + 1], None, ALU.mult, reads=[cst, Rr], writes=[SelR])
                    for hh in range(4):
                        p.mm(PB[0][0:4, hh * 128:(hh + 1) * 128], SelR[:, hh, :], Osb[:, hh, 0:128], True, True, reads=[SelR, Osb], writes=[PB[0]], inc=(hh == 3))
                    p.v("tensor_copy", ats[:], PB[0][0:4, :], reads=[PB[0]], writes=[ats])
                    for hh in range(4):
                        p.act(ats2[:, hh * 128:(hh + 1) * 128], ats[:, hh * 128:(hh + 1) * 128], AF.Square, reads=[ats], writes=[ats2, sms],
                              accum_out=sms[:, hh:hh + 1])
                    rstd_calc(sms[:, 0:4], sms[:, 0:4], 128, 1e-5, [sms])
                    for hh in range(4):
                        p.v("scalar_tensor_tensor", ans[:, hh * 128:(hh + 1) * 128], ats[:, hh * 128:(hh + 1) * 128], sms[:, hh:hh + 1], sgbc[0:4, :],
                            ALU.mult, ALU.mult, reads=[ats, sms, sgbc], writes=[ans])
                    for hh in range(4):
                        p.tr(PT[:, hh * 4:hh * 4 + 4], ans[:, hh * 128:(hh + 1) * 128], identb[0:4, 0:4], reads=[ans, identb], writes=[PT], inc=(hh == 3))
                    p.v("tensor_copy", mx[:, 0:4, 4 * b:4 * b + 4], PT[:, 0:16].rearrange("q (h t) -> q h t", h=4), reads=[PT], writes=[mx])
                out_proj(t, 0, mx, lambda k, s, n: mx[:, k, 0:n], Wo, 8)
                post_norm_residual(t, 0, hb, gpost, tb, ss2)
                store_h(t, hb, False)
            p.barrier()

        only = cfg.get("ONLY", ("even", "odd", "ffn"))
        for l in range(DEPTH):
            if l % 2 == 0:
                if "even" in only:
                    even_phase(l)
            else:
                if "odd" in only:
                    odd_phase(l)
            if "ffn" in only:
                ffn_phase(l, l == DEPTH - 1)
        p.final_wait()
        n_ins = p.n_ins
    return nc, n_ins


def make_consts():
    c = np.zeros((128, 512), np.float32)
    c[:, 0:128] = np.eye(128, dtype=np.float32)
    c[:, 128] = np.arange(128, dtype=np.float32)
    kk = np.arange(128)[:, None]; qq = np.arange(128)[None, :]
    c[:, 129:257] = (kk <= qq).astype(np.float32)
    m4 = (np.arange(4)[:, None] <= np.arange(4)[None, :]).astype(np.float32)
    c[0:4, 257:289] = np.tile(m4, (1, 8))
    c[0:4, 289] = 1.0
    c[4:8, 290] = 1.0
    c[0:8, 291:295] = np.tile(np.eye(4, dtype=np.float32), (2, 1))
    return c


def run(cfg, inputs, trace=False):
    NC, NB, SEQ, DEC_B, NPAGES, NPHYS, DEPTH = (cfg[k] for k in ("NC", "NB", "SEQ", "DEC_B", "NPAGES", "NPHYS", "DEPTH"))
    NPS = NB // NC; NSS = DEC_B // NC; NST = NSS * 4
    NE = (DEPTH + 1) // 2; NO = DEPTH // 2
    nc, n_ins = build(cfg)
    f = lambda a: np.ascontiguousarray(np.asarray(a, dtype=np.float32))
    I = inputs
    shared = {
        "ck": f(I["cache_k"]).reshape(NE * NPHYS * 128, 512),
        "cv": f(I["cache_v"]).reshape(NE * NPHYS * 128, 512),
        "g_mix_pre": f(I["g_mix_pre"]), "g_mix_post": f(I["g_mix_post"]), "g_ffn_pre": f(I["g_ffn_pre"]), "g_ffn_post": f(I["g_ffn_post"]),
        "w_in_even": f(I["w_in_even"]), "lambda_qk": f(I["lambda_qk"]).reshape(NE, 256), "subln_g": f(I["subln_g"]),
        "conv_w": f(I["conv_w"]), "conv_b": f(I["conv_b"]).reshape(NE, 4, 128), "conv_ln_g": f(I["conv_ln_g"]).reshape(NE, 4, 128),
        "conv_ln_b": f(I["conv_ln_b"]).reshape(NE, 4, 128), "w_out_even": f(I["w_out_even"]),
        "w_in_odd": f(I["w_in_odd"]), "ssm_a_re": f(I["ssm_a_re"]).reshape(NO, 32, 128), "ssm_a_im": f(I["ssm_a_im"]).reshape(NO, 32, 128),
        "ssm_log_dt_rep": np.ascontiguousarray(np.repeat(f(I["ssm_log_dt"])[:, :, None], 64, axis=2)).reshape(NO, 32, 128),
        "ssm_b_re": f(I["ssm_b_re"]), "ssm_b_im": f(I["ssm_b_im"]), "ssm_c_re": f(I["ssm_c_re"]), "ssm_c_im": f(I["ssm_c_im"]),
        "ssm_d": f(I["ssm_d"]).reshape(NO, 8, 128), "w_gate_odd": f(I["w_gate_odd"]), "w_out_odd": f(I["w_out_odd"]),
        "w_ffn_up": f(I["w_ffn_up"]), "w_ffn_down": f(I["w_ffn_down"]), "consts": make_consts(),
    }
    xpf = f(I["x_prompt"]); xsf = f(I["x_sample"])
    pt = np.asarray(I["page_table"], dtype=np.int32)
    scv = f(I["state_conv"]); sr = f(I["state_ssm_re"]); si = f(I["state_ssm_im"])
    in_maps = []
    for c in range(NC):
        m = dict(shared)
        m["xp"] = np.ascontiguousarray(xpf[c * NPS:(c + 1) * NPS].reshape(NPS * SEQ, D))
        m["xs"] = np.ascontiguousarray(xsf[c * NSS:(c + 1) * NSS].reshape(NST, D))
        m["pt"] = np.ascontiguousarray(pt[c * NSS:(c + 1) * NSS].reshape(1, NSS * NPAGES))
        m["sconv"] = np.ascontiguousarray(scv[:, c * NSS:(c + 1) * NSS])
        m["sre"] = np.ascontiguousarray(sr[:, c * NSS:(c + 1) * NSS].reshape(NO, NSS * 32, 128))
        m["sim"] = np.ascontiguousarray(si[:, c * NSS:(c + 1) * NSS].reshape(NO, NSS * 32, 128))
        in_maps.append(m)
    res = run_bass_kernel_spmd(nc, in_maps, core_ids=list(range(NC)), **({"trace": True} if trace else {}))
    R = res.results
    cat = lambda name, ax: np.concatenate([R[c][name] for c in range(NC)], axis=ax)
    y_p = cat("y_p", 0).reshape(NB, SEQ, D)
    y_s = cat("y_s", 0).reshape(DEC_B, 4, D)
    nk_p = cat("nk_p", 1).reshape(NE, NB, SEQ, 4, 2, 64)
    nv_p = cat("nv_p", 1).reshape(NE, NB, SEQ, 4, 128)
    nk_s = cat("nk_s", 1).reshape(NE, DEC_B, 4, 4, 2, 64)
    nv_s = cat("nv_s", 1).reshape(NE, DEC_B, 4, 4, 128)
    ncv_p = cat("ncv_p", 1)
    ncv_s = cat("ncv_s", 1)
    sr_p = cat("sr_p", 1).reshape(NO, NB, 64, 64)
    si_p = cat("si_p", 1).reshape(NO, NB, 64, 64)
    sr_s = cat("sr_s", 1).reshape(NO, DEC_B, 64, 64)
    si_s = cat("si_s", 1).reshape(NO, DEC_B, 64, 64)
    outs = (y_p, y_s, nk_p, nv_p, nk_s, nv_s, ncv_p, ncv_s, sr_p, si_p, sr_s, si_s)
    return tuple(np.ascontiguousarray(o, dtype=np.float32) for o in outs), res


def kernel(**inputs):
    outs, _ = run(FULL_CFG, inputs)
    return outs
```

```python
import math
import os
import numpy as np
from contextlib import ExitStack
import concourse.bass as bass
import concourse.mybir as mybir
from concourse.bass_utils import run_bass_kernel_spmd

F32 = mybir.dt.float32
BF16 = mybir.dt.bfloat16
I32 = mybir.dt.int32
AF = mybir.ActivationFunctionType
ALU = mybir.AluOpType
AX = mybir.AxisListType

NDMA = 40
PENG = os.environ.get("PENG", "dve")
PENG2 = os.environ.get("PENG2", "pool")
D = 1024
DFF = 4096
TT = 256
PI = math.pi

FULL_CFG = dict(NC=4, NB=4, SEQ=4096, DEC_B=32, NPAGES=64, NPHYS=2560, DEPTH=4)


class T:
    __slots__ = ("w", "r")

    def __init__(self):
        self.w = None
        self.r = {}


class Buf:
    def __init__(self, t, psum=False):
        self.t = t
        self.T = T()
        self.psum = psum

    def __getitem__(self, k):
        return self.t[k]


class Prog:
    CE = ["pe", "act", "dve", "pool"]

    def __init__(self, nc, es):
        self.nc = nc
        self.eng = {"pe": nc.tensor, "act": nc.scalar, "dve": nc.vector, "pool": nc.gpsimd, "sp": nc.sync}
        self.sem = {}
        for e in self.CE:
            self.sem[e] = es.enter_context(nc.semaphore("s_" + e))
        for k in range(NDMA):
            self.sem[("dma", k)] = es.enter_context(nc.semaphore("s_dma%d" % k))
        self.cnt = {k: 0 for k in self.sem}
        self.seen = {e: {} for e in self.eng}
        self.rr = {"sp": 0, "pool": 0}
        self.n_ins = 0

    def emit(self, engine, fn, reads=(), writes=(), inc=True, dma=False):
        if engine != "pe":
            ex = [b for b in reads if b.psum]
            if ex:
                reads = [b for b in reads if not b.psum]
                writes = list(writes) + [b for b in ex if b not in writes]
        waits = {}

        def add(ev):
            if ev is None:
                return
            k, v = ev
            if engine == "pe" and k == "pe":
                return
            if waits.get(k, 0) < v:
                waits[k] = v

        for b in reads:
            add(b.T.w)
        for b in writes:
            add(b.T.w)
            for k, v in b.T.r.items():
                add((k, v))
        seen = self.seen[engine]
        e = self.eng[engine]
        for k, v in waits.items():
            if seen.get(k, 0) >= v:
                continue
            seen[k] = v
            e.wait_ge(self.sem[k], v)
        if dma:
            NSP = 24
            if engine == "sp":
                k = ("dma", self.rr["sp"]); self.rr["sp"] = (self.rr["sp"] + 1) % NSP
            else:
                k = ("dma", NSP + self.rr["pool"]); self.rr["pool"] = (self.rr["pool"] + 1) % (NDMA - NSP)
            if self.cnt[k] > 0 and seen.get(k, 0) < self.cnt[k]:
                seen[k] = self.cnt[k]
                e.wait_ge(self.sem[k], self.cnt[k])
        ins = fn(e)
        self.n_ins += 1
        if dma:
            self.cnt[k] += 16
            ins.then_inc(self.sem[k], 16)
            ev = (k, self.cnt[k])
        elif inc:
            self.cnt[engine] += 1
            ins.then_inc(self.sem[engine], 1)
            ev = (engine, self.cnt[engine])
        else:
            ev = (engine, self.cnt[engine] + 1)
        for b in writes:
            b.T.w = ev
            b.T.r = {}
        for b in reads:
            k, v = ev
            if b.T.r.get(k, 0) < v:
                b.T.r[k] = v
        return ins

    def barrier(self):
        for engine, e in self.eng.items():
            seen = self.seen[engine]
            for k, v in self.cnt.items():
                if v == 0 or seen.get(k, 0) >= v or engine == k:
                    continue
                seen[k] = v
                e.wait_ge(self.sem[k], v)

    def final_wait(self):
        e = self.eng["sp"]
        seen = self.seen["sp"]
        for k, v in self.cnt.items():
            if v == 0 or seen.get(k, 0) >= v:
                continue
            seen[k] = v
            e.wait_ge(self.sem[k], v)

    def dma(self, out, in_, reads=(), writes=(), engine="sp", **kw):
        return self.emit(engine, lambda e: e.dma_start(out=out, in_=in_, **kw), reads, writes, dma=True)

    def act(self, out, in_, func, reads=(), writes=(), **kw):
        return self.emit("act", lambda e: e.activation(out=out, in_=in_, func=func, **kw), reads, writes)

    def mm(self, out, lhsT, rhs, start, stop, reads=(), writes=(), inc=None, **kw):
        if inc is None:
            inc = stop
        return self.emit("pe", lambda e: e.matmul(out, lhsT, rhs, start=start, stop=stop, **kw), reads, writes, inc=inc)

    def tr(self, out, in_, ident, reads=(), writes=(), inc=True):
        return self.emit("pe", lambda e: e.transpose(out, in_, ident), reads, writes, inc=inc)

    def v(self, name, *args, reads=(), writes=(), engine="dve", **kw):
        return self.emit(engine, lambda e: getattr(e, name)(*args, **kw), reads, writes)


def lam_init_fn(layer):
    return 0.8 - 0.6 * math.exp(-0.3 * layer)


def build(cfg):
    NC, NB, SEQ, DEC_B, NPAGES, NPHYS, DEPTH = (cfg[k] for k in ("NC", "NB", "SEQ", "DEC_B", "NPAGES", "NPHYS", "DEPTH"))
    NPS = NB // NC
    NSS = DEC_B // NC
    NST = NSS * 4
    NE = (DEPTH + 1) // 2
    NO = DEPTH // 2
    NTP = NPS * SEQ
    NTILE = SEQ // TT
    NBLK = SEQ // 128
    assert NST <= 128 and SEQ % TT == 0

    nc = bass.Bass("TRN2", target_bir_lowering=False)

    def din(name, shape, dt=F32):
        return nc.dram_tensor(name, list(shape), dt, kind="ExternalInput")

    def dout(name, shape, dt=F32):
        return nc.dram_tensor(name, list(shape), dt, kind="ExternalOutput")

    def dscr(name, shape, dt=F32):
        return nc.dram_tensor(name, list(shape), dt, kind="Internal")

    xp = din("xp", [NTP, D]).ap()
    xs = din("xs", [NST, D]).ap()
    ck = din("ck", [NE * NPHYS * 128, 512]).ap()
    cv = din("cv", [NE * NPHYS * 128, 512]).ap()
    pt_h = din("pt", [1, NSS * NPAGES], I32)
    sconv = din("sconv", [NE, NSS, 30, 512]).ap()
    sre = din("sre", [max(NO, 1), NSS * 32, 128]).ap()
    sim = din("sim", [max(NO, 1), NSS * 32, 128]).ap()
    g_h = {k: din(k, [DEPTH, D]) for k in ("g_mix_pre", "g_mix_post", "g_ffn_pre", "g_ffn_post")}
    w_in_even = din("w_in_even", [NE, D, 2560]).ap()
    lqk_h = din("lambda_qk", [NE, 256])
    subln_h = din("subln_g", [NE, 128])
    conv_w = din("conv_w", [NE, 31, 512]).ap()
    conv_b = din("conv_b", [NE, 4, 128]).ap()
    ln_g = din("conv_ln_g", [NE, 4, 128]).ap()
    ln_b = din("conv_ln_b", [NE, 4, 128]).ap()
    w_out_even = din("w_out_even", [NE, D, D]).ap()
    w_in_odd = din("w_in_odd", [max(NO, 1), D, D]).ap()
    a_re_d = din("ssm_a_re", [max(NO, 1), 32, 128]).ap()
    a_im_d = din("ssm_a_im", [max(NO, 1), 32, 128]).ap()
    ldt_d = din("ssm_log_dt_rep", [max(NO, 1), 32, 128]).ap()
    b_re_d = din("ssm_b_re", [max(NO, 1), 64, 64, 16]).ap()
    b_im_d = din("ssm_b_im", [max(NO, 1), 64, 64, 16]).ap()
    c_re_d = din("ssm_c_re", [max(NO, 1), 64, 16, 64]).ap()
    c_im_d = din("ssm_c_im", [max(NO, 1), 64, 16, 64]).ap()
    ssm_d = din("ssm_d", [max(NO, 1), 8, 128]).ap()
    w_gate_odd = din("w_gate_odd", [max(NO, 1), D, D]).ap()
    w_out_odd = din("w_out_odd", [max(NO, 1), D, D]).ap()
    w_up = din("w_ffn_up", [DEPTH, D, DFF]).ap()
    w_dn = din("w_ffn_down", [DEPTH, DFF, D]).ap()
    consts = din("consts", [128, 512]).ap()

    y_p = dout("y_p", [NTP, D]).ap()
    y_s = dout("y_s", [NST, D]).ap()
    nk_p = dout("nk_p", [NE, NTP, 512]).ap()
    nv_p = dout("nv_p", [NE, NTP, 512]).ap()
    nk_s = dout("nk_s", [NE, NST, 512]).ap()
    nv_s = dout("nv_s", [NE, NST, 512]).ap()
    ncv_p = dout("ncv_p", [NE, NPS, 30, 512]).ap()
    ncv_s = dout("ncv_s", [NE, NSS, 30, 512]).ap()
    sr_p = dout("sr_p", [max(NO, 1), NPS * 32, 128]).ap()
    si_p = dout("si_p", [max(NO, 1), NPS * 32, 128]).ap()
    sr_s = dout("sr_s", [max(NO, 1), NSS * 32, 128]).ap()
    si_s = dout("si_s", [max(NO, 1), NSS * 32, 128]).ap()

    Hp = dscr("Hp", [NTP, D]).ap()
    Hs = dscr("Hs", [NST, D]).ap()
    QT = dscr("QT", [128, 4, SEQ], BF16).ap()
    CT = dscr("CT", [128, 4, SEQ], BF16).ap()
    DT = {"Hp": Buf(None), "Hs": Buf(None), "QT": Buf(None), "CT": Buf(None), "out": Buf(None)}

    def bcast(handle, off, n):
        return bass.AP(handle, off, [[0, 128], [1, n]])

    with ExitStack() as es0:
        p = Prog(nc, es0)

        uid = [0]

        def SB(es, name, shape, dt=F32):
            uid[0] += 1
            return Buf(es.enter_context(nc.sbuf_tensor("%s_%d" % (name, uid[0]), list(shape), dt)))

        def PSB(es, name, shape, dt=F32):
            return Buf(es.enter_context(nc.psum_tensor(name, list(shape), dt)), psum=True)

        PT = PSB(es0, "PT", [128, 1024], BF16)
        PA = [PSB(es0, "PA%d" % i, [128, 512]) for i in range(2)]
        PB = [PSB(es0, "PB%d" % i, [128, 512]) for i in range(2)]
        PO = [PSB(es0, "PO%d" % i, [128, 512]) for i in range(2)]
        PX = PSB(es0, "PX", [128, 512])

        cst = SB(es0, "cst", [128, 512])
        p.dma(cst[:], consts[:, :], writes=[cst])
        identf = cst[:, 0:128]
        iota_p = cst[:, 128:129]
        tri_f = cst[:, 129:257]
        mask4_f = cst[0:4, 257:289]
        cm0 = cst[0:8, 289:290]
        cm1 = cst[0:8, 290:291]
        sel_f = cst[0:8, 291:295]
        identb = SB(es0, "identb", [128, 128], BF16)
        p.v("tensor_copy", identb[:], identf, reads=[cst], writes=[identb])
        trib = SB(es0, "trib", [128, 128], BF16)
        p.v("tensor_copy", trib[:], tri_f, reads=[cst], writes=[trib])
        onesf = SB(es0, "onesf", [128, 128])
        p.v("memset", onesf[:], 1.0 / 512.0, writes=[onesf])
        small = SB(es0, "small", [128, 64])
        gidx = SB(es0, "gidx", [128, NE, NSS * NPAGES], I32)

        with ExitStack() as es:
            pti = SB(es, "pti", [128, NSS * NPAGES], I32)
            ptf = SB(es, "ptf", [128, NSS * NPAGES])
            p.dma(pti[:], bcast(pt_h, 0, NSS * NPAGES), writes=[pti])
            p.v("tensor_copy", ptf[:], pti[:], reads=[pti], writes=[ptf])
            p.v("tensor_scalar", ptf[:], ptf[:], 128.0, None, ALU.mult, reads=[ptf], writes=[ptf])
            p.v("tensor_scalar", ptf[:], ptf[:], iota_p, None, ALU.add, reads=[ptf, cst], writes=[ptf])
            for i in range(NE):
                if i > 0:
                    p.v("tensor_scalar", ptf[:], ptf[:], float(NPHYS * 128), None, ALU.add, reads=[ptf], writes=[ptf])
                p.v("tensor_copy", gidx[:, i, :], ptf[:], reads=[ptf], writes=[gidx])
            p.barrier()

        p.dma(Hp[:, :], xp[:, :], writes=[DT["Hp"]])
        p.dma(Hs[:, :], xs[:, :], writes=[DT["Hs"]])

        def rstd_calc(out_ap, in_ap, n_dim, eps, bufs):
            p.v("tensor_scalar", out_ap, in_ap, 1.0 / n_dim, float(eps), ALU.mult, ALU.add, reads=bufs, writes=bufs)
            p.act(out_ap, out_ap, AF.Ln, reads=bufs, writes=bufs)
            p.act(out_ap, out_ap, AF.Exp, reads=bufs, writes=bufs, scale=-0.5)

        def load_weight(W, src, nk, ncol):
            for k in range(nk):
                for c0 in range(0, ncol, 2048):
                    c1 = min(ncol, c0 + 2048)
                    p.dma(W[:, k, c0:c1], src[k * 128:(k + 1) * 128, c0:c1], writes=[W], engine="pool")

        class Tile:
            pass

        def make_tiles(tt=TT):
            tiles = []
            for sq in range(NPS):
                for ti in range(SEQ // tt):
                    t = Tile()
                    t.kind = "p"; t.seq = sq; t.ti = ti; t.ntok = tt; t.np = 128; t.nsub = tt // 128
                    t.row0 = sq * SEQ + ti * tt
                    t.first = ti == 0; t.last = ti == SEQ // tt - 1
                    tiles.append(t)
            t = Tile()
            t.kind = "s"; t.seq = 0; t.ti = 0; t.ntok = NST; t.np = NST; t.nsub = 1; t.row0 = 0
            t.first = True; t.last = True
            tiles.append(t)
            return tiles

        def h_src(t):
            H = Hp if t.kind == "p" else Hs
            if t.kind == "p":
                return H[t.row0:t.row0 + t.ntok, :].rearrange("(s q) d -> q s d", q=128), DT["Hp"]
            return H[0:NST, :].rearrange("(s q) d -> q s d", s=1), DT["Hs"]

        def load_h(t, hb):
            src, trk = h_src(t)
            p.dma(hb[0:t.np, 0:t.nsub, :], src, reads=[trk], writes=[hb])

        def store_h(t, hb, final):
            if final:
                Y = y_p if t.kind == "p" else y_s
                if t.kind == "p":
                    dst = Y[t.row0:t.row0 + t.ntok, :].rearrange("(s q) d -> q s d", q=128)
                else:
                    dst = Y[0:NST, :].rearrange("(s q) d -> q s d", s=1)
                p.dma(dst, hb[0:t.np, 0:t.nsub, :], reads=[hb], writes=[DT["out"]])
            else:
                dst, trk = h_src(t)
                p.dma(dst, hb[0:t.np, 0:t.nsub, :], reads=[hb], writes=[trk])

        def norm_to_xT(t, hb, gbc, xn, xT, ss):
            n = t.np
            for s in range(t.nsub):
                p.act(xn[0:n, s, :], hb[0:n, s, :], AF.Square, reads=[hb], writes=[xn, ss], accum_out=ss[0:n, s:s + 1])
            rstd_calc(ss[0:n, 0:t.nsub], ss[0:n, 0:t.nsub], D, 1e-6, [ss])
            for s in range(t.nsub):
                p.v("scalar_tensor_tensor", xn[0:n, s, :], hb[0:n, s, :], ss[0:n, s:s + 1], gbc[0:n, :], ALU.mult, ALU.mult,
                    reads=[hb, ss, gbc], writes=[xn])
            for s in range(t.nsub):
                for k in range(8):
                    p.tr(PT[:, k * 128:k * 128 + n], xn[0:n, s, k * 128:(k + 1) * 128], identb[0:n, 0:n],
                         reads=[xn, identb], writes=[PT], inc=(k == 7))
                p.v("tensor_copy", xT[:, :, s * 128:s * 128 + n], PT[:, :].rearrange("q (k c) -> q k c", k=8)[:, :, 0:n],
                    reads=[PT], writes=[xT])

        def post_norm_residual(t, s, hb, gbc, tb, ss2):
            n = t.np
            for half in range(2):
                p.act(tb[0:n, half * 512:(half + 1) * 512], PB[half][0:n, :], AF.Square, reads=[PB[half]], writes=[tb, ss2],
                      accum_out=ss2[0:n, half:half + 1])
            p.v("tensor_tensor", ss2[0:n, 2:3], ss2[0:n, 0:1], ss2[0:n, 1:2], ALU.add, reads=[ss2], writes=[ss2])
            rstd_calc(ss2[0:n, 2:3], ss2[0:n, 2:3], D, 1e-6, [ss2])
            for half in range(2):
                p.v("scalar_tensor_tensor", tb[0:n, half * 512:(half + 1) * 512], PB[half][0:n, :], ss2[0:n, 2:3],
                    gbc[0:n, half * 512:(half + 1) * 512], ALU.mult, ALU.mult, reads=[PB[half], ss2, gbc], writes=[tb])
            p.v("tensor_tensor", hb[0:n, s, :], hb[0:n, s, :], tb[0:n, :], ALU.add, reads=[hb, tb], writes=[hb], engine=os.environ.get("RESENG", "dve"))

        def out_proj(t, s, lhs_buf, lhs_fn, W, nk):
            n = t.np
            for half in range(2):
                for k in range(nk):
                    p.mm(PB[half][0:n, :], lhs_fn(k, s, n), W[:, k, half * 512:(half + 1) * 512], k == 0, k == nk - 1,
                         reads=[lhs_buf, W], writes=[PB[half]])

        def ffn_phase(l, final):
            with ExitStack() as es:
                Wup = SB(es, "Wup", [128, 8, DFF], BF16)
                Wdn = SB(es, "Wdn", [128, 32, D], BF16)
                gpre = SB(es, "gpre", [128, D]); gpost = SB(es, "gpost", [128, D])
                hbs = [SB(es, "hb%d" % i, [128, 2, D]) for i in range(2)]
                xn = SB(es, "xn", [128, 2, D], BF16)
                xT = SB(es, "xT", [128, 8, TT], BF16)
                hid = SB(es, "hid", [128, 32, TT], BF16)
                rb = [SB(es, "rb%d" % i, [128, TT], BF16) for i in range(2)]
                tb = SB(es, "tb", [128, D])
                ss = SB(es, "ss", [128, 4]); ss2 = SB(es, "ss2", [128, 4])
                p.dma(gpre[:], bcast(g_h["g_ffn_pre"], l * D, D), writes=[gpre])
                p.dma(gpost[:], bcast(g_h["g_ffn_post"], l * D, D), writes=[gpost])
                load_weight(Wup, w_up[l], 8, DFF)
                load_weight(Wdn, w_dn[l], 32, D)
                import os
                DBG = int(os.environ.get("DBGSTEP", "9"))
                for it, t in enumerate(make_tiles()):
                    hb = hbs[it % 2]
                    nt = t.ntok
                    if DBG < 1:
                        continue
                    load_h(t, hb)
                    if DBG < 2:
                        continue
                    norm_to_xT(t, hb, gpre, xn, xT, ss)
                    if DBG < 3:
                        continue
                    for m in range(32):
                        pa = PA[m % 2]
                        for k in range(8):
                            p.mm(pa[:, 0:nt], Wup[:, k, m * 128:(m + 1) * 128], xT[:, k, 0:nt], k == 0, k == 7,
                                 reads=[Wup, xT], writes=[pa])
                        r = rb[m % 2]
                        p.act(r[:, 0:nt], pa[:, 0:nt], AF.Relu, reads=[pa], writes=[r])
                        p.v("tensor_tensor", hid[:, m, 0:nt], r[:, 0:nt], r[:, 0:nt], ALU.mult, reads=[r], writes=[hid], engine=PENG2)
                    if DBG < 4:
                        continue
                    for s in range(t.nsub):
                        out_proj(t, s, hid, lambda k, s, n: hid[:, k, s * 128:s * 128 + n], Wdn, 32)
                        if DBG >= 5:
                            post_norm_residual(t, s, hb, gpost, tb, ss2)
                    store_h(t, hb, final)
                p.barrier()

        def odd_phase(l):
            i = l // 2
            TO = 128
            with ExitStack() as es:
                gpre = SB(es, "gpre", [128, D]); gpost = SB(es, "gpost", [128, D])
                LB = [SB(es, "LB%d" % r, [128, 32, 128], BF16) for r in range(2)]
                LC = [SB(es, "LC%d" % r, [128, 32, 128], BF16) for r in range(2)]
                U = [SB(es, "U%d" % r, [128, 32, TO]) for r in range(2)]
                st = SB(es, "st", [128, 20, 32])
                dsk = SB(es, "dsk", [128, 8])
                A_RE, A_IM, MAG, UL_RE, UL_IM, U1_RE, U1_IM, ULM_RE, ULM_IM = 0, 1, 2, 3, 4, 5, 6, 7, 8
                F_RE, F_IM, G_RE, G_IM, GL_RE, GL_IM, TMP0, TMP1, TMP2, TMP3 = 9, 10, 11, 12, 13, 14, 15, 16, 17, 18
                p.dma(gpre[:], bcast(g_h["g_mix_pre"], l * D, D), writes=[gpre])
                p.dma(gpost[:], bcast(g_h["g_mix_post"], l * D, D), writes=[gpost])
                p.dma(dsk[:], ssm_d[i].rearrange("j q -> q j"), writes=[dsk], allow_slow_non_contiguous=True)

                stG = Buf(st.t)
                stGL = Buf(st.t)

                def S(k):
                    return st[:, k, :]

                def sopx(name, *a, **kw):
                    p.v(name, *a, reads=[st, stG, stGL], writes=[st, stG], **kw)

                def sop(name, *a, **kw):
                    p.v(name, *a, reads=[st], writes=[st], **kw)

                OS = int(os.environ.get("ODDSTEP", "9"))
                with ExitStack() as es2:
                    ld = SB(es2, "ld", [32, 3, 128])
                    p.dma(ld[:, 0, :], a_re_d[i], writes=[ld]); p.dma(ld[:, 1, :], a_im_d[i], writes=[ld]); p.dma(ld[:, 2, :], ldt_d[i], writes=[ld])
                    for r, dst in enumerate((TMP0, TMP1, TMP2)):
                        p.tr(PX[:, 0:32], ld[:, r, :], identf[0:32, 0:32], reads=[ld, cst], writes=[PX])
                        p.v("tensor_copy", S(dst), PX[:, 0:32], reads=[PX], writes=[st])
                    lr, li, dtt = S(TMP0), S(TMP1), S(TMP2)
                    p.act(dtt, dtt, AF.Exp, reads=[st], writes=[st])
                    sop("tensor_tensor", S(MAG), lr, dtt, ALU.mult)
                    p.act(S(MAG), S(MAG), AF.Exp, reads=[st], writes=[st])
                    sop("tensor_tensor", S(TMP3), li, dtt, ALU.mult)

                    def sincos(dst, shift):
                        x = S(G_RE); k = S(G_IM); ki = S(GL_RE)
                        sop("tensor_scalar", x, S(TMP3), float(shift), None, ALU.add)
                        sop("tensor_scalar", k, x, 1.0 / (2 * PI), None, ALU.mult)
                        p.v("tensor_copy", st[:, GL_RE, :].bitcast(I32), k, reads=[st], writes=[st])
                        p.v("tensor_copy", k, st[:, GL_RE, :].bitcast(I32), reads=[st], writes=[st])
                        sop("scalar_tensor_tensor", x, k, -2 * PI, x, ALU.mult, ALU.add)
                        sop("tensor_scalar", k, x, PI, -2 * PI, ALU.is_gt, ALU.mult)
                        sop("tensor_tensor", x, x, k, ALU.add)
                        sop("tensor_scalar", k, x, -PI, 2 * PI, ALU.is_lt, ALU.mult)
                        sop("tensor_tensor", x, x, k, ALU.add)
                        p.act(dst, x, AF.Sin, reads=[st], writes=[st])

                    sincos(S(U1_IM), 0.0)
                    sincos(S(U1_RE), PI / 2)
                    sop("tensor_tensor", S(A_RE), S(MAG), S(U1_RE), ALU.mult)
                    sop("tensor_tensor", S(A_IM), S(MAG), S(U1_IM), ALU.mult)
                    den = S(G_RE); nr = S(G_IM); t0_ = S(GL_RE); t1_ = S(GL_IM)
                    sop("tensor_tensor", den, lr, lr, ALU.mult)
                    sop("tensor_tensor", t0_, li, li, ALU.mult)
                    sop("tensor_tensor", den, den, t0_, ALU.add)
                    sop("reciprocal", den, den)
                    sop("tensor_scalar", nr, S(A_RE), -1.0, None, ALU.add)
                    sop("tensor_tensor", t0_, nr, lr, ALU.mult)
                    sop("tensor_tensor", t1_, S(A_IM), li, ALU.mult)
                    sop("tensor_tensor", t0_, t0_, t1_, ALU.add)
                    sop("tensor_tensor", S(F_RE), t0_, den, ALU.mult)
                    sop("tensor_tensor", t0_, S(A_IM), lr, ALU.mult)
                    sop("tensor_tensor", t1_, nr, li, ALU.mult)
                    sop("tensor_tensor", t0_, t0_, t1_, ALU.subtract)
                    sop("tensor_tensor", S(F_IM), t0_, den, ALU.mult)
                    p.v("memset", U[0][:, :, 0:1], 1.0, writes=[U[0]])
                    p.v("memset", U[1][:, :, 0:1], 0.0, writes=[U[1]])
                    sop("tensor_copy", S(UL_RE), S(U1_RE)); sop("tensor_copy", S(UL_IM), S(U1_IM))
                    n = 1 if OS >= 2 else TO
                    tmpa = SB(es2, "tmpa", [128, 32, TO // 2]); tmpb = SB(es2, "tmpb", [128, 32, TO // 2])
                    while n < TO:
                        cr = st[:, UL_RE, :].unsqueeze(2).to_broadcast([128, 32, n])
                        ci = st[:, UL_IM, :].unsqueeze(2).to_broadcast([128, 32, n])
                        a0 = U[0][:, :, 0:n]; a1 = U[1][:, :, 0:n]
                        p.v("tensor_tensor", tmpa[:, :, 0:n], a0, cr, ALU.mult, reads=[U[0], st], writes=[tmpa])
                        p.v("tensor_tensor", tmpb[:, :, 0:n], a1, ci, ALU.mult, reads=[U[1], st], writes=[tmpb])
                        p.v("tensor_tensor", U[0][:, :, n:2 * n], tmpa[:, :, 0:n], tmpb[:, :, 0:n], ALU.subtract, reads=[tmpa, tmpb], writes=[U[0]])
                        p.v("tensor_tensor", tmpa[:, :, 0:n], a0, ci, ALU.mult, reads=[U[0], st], writes=[tmpa])
                        p.v("tensor_tensor", tmpb[:, :, 0:n], a1, cr, ALU.mult, reads=[U[1], st], writes=[tmpb])
                        p.v("tensor_tensor", U[1][:, :, n:2 * n], tmpa[:, :, 0:n], tmpb[:, :, 0:n], ALU.add, reads=[tmpa, tmpb], writes=[U[1]])
                        sop("tensor_tensor", t0_, S(UL_RE), S(UL_RE), ALU.mult)
                        sop("tensor_tensor", t1_, S(UL_IM), S(UL_IM), ALU.mult)
                        sop("tensor_tensor", nr, S(UL_RE), S(UL_IM), ALU.mult)
                        sop("tensor_tensor", S(UL_RE), t0_, t1_, ALU.subtract)
                        sop("tensor_scalar", S(UL_IM), nr, 2.0, None, ALU.mult)
                        n *= 2
                    p.v("tensor_copy", S(ULM_RE), U[0][:, :, TO - 1], reads=[U[0], st], writes=[st])
                    p.v("tensor_copy", S(ULM_IM), U[1][:, :, TO - 1], reads=[U[1], st], writes=[st])
                    p.barrier()
                with ExitStack() as es2:
                  if OS >= 3:
                    Bs = [SB(es2, "Bs%d" % r, [128, 32, 128]) for r in range(2)]
                    Cs = [SB(es2, "Cs%d" % r, [128, 32, 128]) for r in range(2)]
                    for r in range(2):
                        p.v("memset", Bs[r][:], 0.0, writes=[Bs[r]], engine=PENG)
                        p.v("memset", Cs[r][:], 0.0, writes=[Cs[r]], engine=PENG)
                    for r, (bd, cd) in enumerate(((b_re_d, c_re_d), (b_im_d, c_im_d))):
                        for two in range(2):
                            for q4 in range(4):
                                bsrc = bd[i].rearrange("(m e) p c -> e p m c", e=8)[2 * q4 + two]
                                bdst = Bs[r][64 * two:64 * two + 64, :, :].rearrange("p (m f) c -> p m f c", f=4)[:, :, q4, 32 * q4 + 16 * two:32 * q4 + 16 * two + 16]
                                p.dma(bdst, bsrc, writes=[Bs[r]])
                                csrc = cd[i].rearrange("(m e) c p -> e c m p", e=8)[2 * q4 + two]
                                c0 = 32 * q4 + 16 * two
                                cdst = Cs[r][c0:c0 + 16, :, :].rearrange("c (m f) p -> c m f p", f=4)[:, :, q4, 64 * two:64 * two + 64]
                                p.dma(cdst, csrc, writes=[Cs[r]])
                    tbb = SB(es2, "tbb", [128, 128]); tbc = SB(es2, "tbc", [128, 128])
                    for s in range(32):
                        fr = st[:, F_RE, s:s + 1]; fi = st[:, F_IM, s:s + 1]
                        p.v("tensor_scalar", tbb[:], Bs[1][:, s, :], fi, None, ALU.mult, reads=[Bs[1], st], writes=[tbb])
                        p.v("scalar_tensor_tensor", tbc[:], Bs[0][:, s, :], fr, tbb[:], ALU.mult, ALU.subtract, reads=[Bs[0], st, tbb], writes=[tbc])
                        p.tr(PX[:, 0:128], tbc[:], identf, reads=[tbc, cst], writes=[PX])
                        p.act(LB[0][:, s, :], PX[:, 0:128], AF.Copy, reads=[PX], writes=[LB[0]])
                        p.v("tensor_scalar", tbb[:], Bs[0][:, s, :], fi, None, ALU.mult, reads=[Bs[0], st], writes=[tbb])
                        p.v("scalar_tensor_tensor", tbc[:], Bs[1][:, s, :], fr, tbb[:], ALU.mult, ALU.add, reads=[Bs[1], st, tbb], writes=[tbc])
                        p.tr(PX[:, 128:256], tbc[:], identf, reads=[tbc, cst], writes=[PX])
                        p.act(LB[1][:, s, :], PX[:, 128:256], AF.Copy, reads=[PX], writes=[LB[1]])
                        p.tr(PX[:, 256:384], Cs[0][:, s, :], identf, reads=[Cs[0], cst], writes=[PX])
                        p.act(LC[0][:, s, :], PX[:, 256:384], AF.Copy, reads=[PX], writes=[LC[0]])
                        p.tr(PX[:, 384:512], Cs[1][:, s, :], identf, reads=[Cs[1], cst], writes=[PX])
                        p.act(LC[1][:, s, :], PX[:, 384:512], AF.Copy, reads=[PX], writes=[LC[1]], scale=-1.0)
                    p.barrier()

                Win = SB(es, "Win", [128, 8, D], BF16)
                Wg = SB(es, "Wg", [128, 8, D], BF16)
                Wo = SB(es, "Wo", [128, 8, D], BF16)
                load_weight(Win, w_in_odd[i], 8, D)
                load_weight(Wg, w_gate_odd[i], 8, D)
                load_weight(Wo, w_out_odd[i], 8, D)
                hbs = [SB(es, "hb%d" % k, [128, 1, D]) for k in range(2)]
                xn = SB(es, "xn", [128, 1, D], BF16)
                xT = SB(es, "xT", [128, 8, TO], BF16)
                uTb = SB(es, "uTb", [128, 8, TO], BF16)
                uTf = SB(es, "uTf", [128, 8, TO])
                zT = SB(es, "zT", [128, 8, TO], BF16)
                zzT = SB(es, "zzT", [128, 8, TO], BF16)
                sgb = [SB(es, "sgb%d" % k, [128, TO], BF16) for k in range(2)]
                tb = SB(es, "tb", [128, D])
                ss = SB(es, "ss", [128, 4]); ss2 = SB(es, "ss2", [128, 4])
                w1 = SB(es, "w1", [128, TO]); w2 = SB(es, "w2", [128, TO])
                bpr = SB(es, "bpr", [128, TO]); bpi = SB(es, "bpi", [128, TO])
                gr = SB(es, "gr", [128, TO]); gi = SB(es, "gi", [128, TO])
                rr = SB(es, "rr", [128, TO])
                wt2 = [[SB(es, "wt%d_%d" % (q, k), [128, TO]) for k in range(8)] for q in range(2)]
                dbl = [(bpr, bpi, gr, gi, rr), tuple(SB(es, "dbl%d" % k, [128, TO]) for k in range(5))]
                hre = [SB(es, "hre%d" % k, [128, TO], BF16) for k in range(2)]
                him = [SB(es, "him%d" % k, [128, TO], BF16) for k in range(2)]
                BU = [SB(es, "BU%d" % r, [128, 32, NST]) for r in range(2)]
                HS = [SB(es, "HS%d" % r, [128, 32, NST], BF16) for r in range(2)]
                hcur = [SB(es, "hcur%d" % r, [128, 32, NSS]) for r in range(2)]
                hnew = [SB(es, "hnew%d" % r, [128, 32, NSS]) for r in range(2)]
                hq = SB(es, "hq", [128, 4, 32, NSS])
                hT = SB(es, "hT", [128, NSS * 32])
                stl = SB(es, "stl", [128, 2, 128])

                for it, t in enumerate(make_tiles(TO)):
                    hb = hbs[it % 2]
                    nt = t.ntok
                    if OS < 5:
                        continue
                    if os.environ.get("TILEK", t.kind) != t.kind:
                        continue
                    load_h(t, hb)
                    norm_to_xT(t, hb, gpre, xn, xT, ss)
                    for m in range(8):
                        pa = PA[m % 2]
                        for k in range(8):
                            p.mm(pa[:, 0:nt], Win[:, k, m * 128:(m + 1) * 128], xT[:, k, 0:nt], k == 0, k == 7, reads=[Win, xT], writes=[pa])
                        p.act(uTb[:, m, 0:nt], pa[:, 0:nt], AF.Copy, reads=[pa], writes=[uTb])
                        p.v("tensor_copy", uTf[:, m, 0:nt], pa[:, 0:nt], reads=[pa], writes=[uTf])
                    if OS < 6:
                        continue

                    def bu_mm(s, banks=None):
                        j = s // 4
                        b0, b1 = banks if banks is not None else (PA[0], PA[1])
                        p.mm(b0[:, 0:nt], LB[0][:, s, :], uTb[:, j, 0:nt], True, True, reads=[LB[0], uTb], writes=[b0])
                        p.mm(b1[:, 0:nt], LB[1][:, s, :], uTb[:, j, 0:nt], True, True, reads=[LB[1], uTb], writes=[b1])

                    def y_mm(s, hr_buf, hr_ap, hi_buf, hi_ap):
                        j = s // 4
                        pb = PB[j % 2]
                        p.mm(pb[:, 0:nt], LC[0][:, s, :], hr_ap, s % 4 == 0, False, reads=[LC[0], hr_buf], writes=[pb])
                        p.mm(pb[:, 0:nt], LC[1][:, s, :], hi_ap, False, s % 4 == 3, reads=[LC[1], hi_buf], writes=[pb])
                        if s % 4 == 3:
                            p.v("scalar_tensor_tensor", w1[:, 0:nt], uTf[:, j, 0:nt], dsk[:, j:j + 1], pb[:, 0:nt], ALU.mult, ALU.add,
                                reads=[uTf, dsk, pb], writes=[w1])
                            p.act(zT[:, j, 0:nt], w1[:, 0:nt], AF.Gelu, reads=[w1], writes=[zT])

                    if t.kind == "p":
                        if t.first:
                            p.v("memset", st[:, G_RE, :], 0.0, reads=[st], writes=[st, stG])
                            p.v("memset", st[:, G_IM, :], 0.0, reads=[st], writes=[st, stG])
                        for s in range(32):
                            pr_, pi_ = (PA[0], PA[1]) if s % 2 == 0 else (PO[0], PO[1])
                            bu_mm(s, (pr_, pi_))
                            ur = U[0][:, s, :]; ui = U[1][:, s, :]
                            bpr, bpi, gr, gi, rr = dbl[s % 2]
                            wt = wt2[s % 2]
                            wa, wb, wc, wd, we, wf, wg, wh = wt
                            p.v("tensor_tensor", wa[:], pr_[:, 0:TO], ur, ALU.mult, reads=[pr_, U[0]], writes=[wa])
                            p.v("tensor_tensor", wb[:], pi_[:, 0:TO], ui, ALU.mult, reads=[pi_, U[1]], writes=[wb])
                            p.v("tensor_tensor", bpr[:], wa[:], wb[:], ALU.add, reads=[wa, wb], writes=[bpr], engine=PENG2)
                            p.v("tensor_tensor", wc[:], pi_[:, 0:TO], ur, ALU.mult, reads=[pi_, U[0]], writes=[wc])
                            p.v("tensor_tensor", wd[:], pr_[:, 0:TO], ui, ALU.mult, reads=[pr_, U[1]], writes=[wd])
                            p.v("tensor_tensor", bpi[:], wc[:], wd[:], ALU.subtract, reads=[wc, wd], writes=[bpi], engine=PENG2)
                            p.act(rr[:], U[0][:, s, :], AF.Identity, reads=[U[0], st], writes=[rr], scale=0.0, bias=st[:, MAG, s:s + 1])
                            p.v("tensor_tensor_scan", gr[:], rr[:], bpr[:], st[:, G_RE, s:s + 1], ALU.mult, ALU.add, reads=[rr, bpr, stG], writes=[gr])
                            p.v("tensor_tensor_scan", gi[:], rr[:], bpi[:], st[:, G_IM, s:s + 1], ALU.mult, ALU.add, reads=[rr, bpi, stG], writes=[gi])
                            p.act(st[:, GL_RE, s:s + 1], gr[:, TO - 1:TO], AF.Copy, reads=[gr], writes=[stGL])
                            p.act(st[:, GL_IM, s:s + 1], gi[:, TO - 1:TO], AF.Copy, reads=[gi], writes=[stGL])
                            hr = hre[s % 2]; hi = him[s % 2]
                            p.v("tensor_tensor", we[:], gr[:], ur, ALU.mult, reads=[gr, U[0]], writes=[we])
                            p.v("tensor_tensor", wf[:], gi[:], ui, ALU.mult, reads=[gi, U[1]], writes=[wf])
                            p.v("tensor_tensor", hr[:], we[:], wf[:], ALU.subtract, reads=[we, wf], writes=[hr], engine=PENG2)
                            p.v("tensor_tensor", wg[:], gr[:], ui, ALU.mult, reads=[gr, U[1]], writes=[wg])
                            p.v("tensor_tensor", wh[:], gi[:], ur, ALU.mult, reads=[gi, U[0]], writes=[wh])
                            p.v("tensor_tensor", hi[:], wg[:], wh[:], ALU.add, reads=[wg, wh], writes=[hi], engine=PENG2)
                            y_mm(s, hr, hr[:], hi, hi[:])
                        if t.last:
                            sopx("tensor_tensor", S(TMP0), S(GL_RE), S(ULM_RE), ALU.mult)
                            sopx("tensor_tensor", S(TMP1), S(GL_IM), S(ULM_IM), ALU.mult)
                            sopx("tensor_tensor", S(TMP2), S(TMP0), S(TMP1), ALU.subtract)
                            sopx("tensor_tensor", S(TMP0), S(GL_RE), S(ULM_IM), ALU.mult)
                            sopx("tensor_tensor", S(TMP1), S(GL_IM), S(ULM_RE), ALU.mult)
                            sopx("tensor_tensor", S(TMP3), S(TMP0), S(TMP1), ALU.add)
                            for r, (src, dstd) in enumerate(((TMP2, sr_p), (TMP3, si_p))):
                                p.tr(PX[0:32, r * 128:(r + 1) * 128], st[:, src, :], identf, reads=[st, cst], writes=[PX])
                                p.act(stl[0:32, r, :], PX[0:32, r * 128:(r + 1) * 128], AF.Copy, reads=[PX], writes=[stl])
                                p.dma(dstd[i, t.seq * 32:(t.seq + 1) * 32, :], stl[0:32, r, :], reads=[stl], writes=[DT["out"]])
                        else:
                            sopx("tensor_tensor", S(TMP0), S(GL_RE), S(UL_RE), ALU.mult)
                            sopx("tensor_tensor", S(TMP1), S(GL_IM), S(UL_IM), ALU.mult)
                            sopx("tensor_tensor", S(G_RE), S(TMP0), S(TMP1), ALU.subtract)
                            sopx("tensor_tensor", S(TMP0), S(GL_RE), S(UL_IM), ALU.mult)
                            sopx("tensor_tensor", S(TMP1), S(GL_IM), S(UL_RE), ALU.mult)
                            sopx("tensor_tensor", S(G_IM), S(TMP0), S(TMP1), ALU.add)
                    else:
                        for s in range(32):
                            bu_mm(s)
                            p.act(BU[0][:, s, :], PA[0][:, 0:nt], AF.Copy, reads=[PA[0]], writes=[BU[0]])
                            p.v("tensor_copy", BU[1][:, s, :], PA[1][:, 0:nt], reads=[PA[1]], writes=[BU[1]])
                        for r, sd in enumerate((sre, sim)):
                            nrow = NSS * 32
                            for c0 in range(0, nrow, 128):
                                c1 = min(nrow, c0 + 128)
                                p.dma(stl[0:c1 - c0, r, :], sd[i, c0:c1, :], writes=[stl])
                                p.tr(PX[:, 0:c1 - c0], stl[0:c1 - c0, r, :], identf[0:c1 - c0, 0:c1 - c0], reads=[stl, cst], writes=[PX])
                                nb = (c1 - c0) // 32
                                p.v("tensor_copy", hcur[r][:, :, c0 // 32:c0 // 32 + nb],
                                    PX[:, 0:c1 - c0].rearrange("q (b s) -> q s b", s=32), reads=[PX], writes=[hcur[r]])
                        are = st[:, A_RE, :].unsqueeze(2).to_broadcast([128, 32, NSS])
                        aim = st[:, A_IM, :].unsqueeze(2).to_broadcast([128, 32, NSS])
                        for tt_ in range(4):
                            bur = BU[0][:, :, :].rearrange("q s (b t) -> q s b t", t=4)[:, :, :, tt_]
                            bui = BU[1][:, :, :].rearrange("q s (b t) -> q s b t", t=4)[:, :, :, tt_]
                            p.v("tensor_tensor", hq[:, 0], hcur[0][:], are, ALU.mult, reads=[hcur[0], st], writes=[hq])
                            p.v("tensor_tensor", hq[:, 1], hcur[1][:], aim, ALU.mult, reads=[hcur[1], st], writes=[hq])
                            p.v("tensor_tensor", hq[:, 2], hcur[0][:], aim, ALU.mult, reads=[hcur[0], st], writes=[hq])
                            p.v("tensor_tensor", hq[:, 3], hcur[1][:], are, ALU.mult, reads=[hcur[1], st], writes=[hq])
                            p.v("tensor_tensor", hq[:, 0], hq[:, 0], hq[:, 1], ALU.subtract, reads=[hq], writes=[hq])
                            p.v("tensor_tensor", hq[:, 2], hq[:, 2], hq[:, 3], ALU.add, reads=[hq], writes=[hq])
                            p.v("tensor_tensor", hnew[0][:], hq[:, 0], bur, ALU.add, reads=[hq, BU[0]], writes=[hnew[0]])
                            p.v("tensor_tensor", hnew[1][:], hq[:, 2], bui, ALU.add, reads=[hq, BU[1]], writes=[hnew[1]])
                            for r in range(2):
                                p.v("tensor_copy", HS[r][:, :, :].rearrange("q s (b t) -> q s b t", t=4)[:, :, :, tt_], hnew[r][:],
                                    reads=[hnew[r]], writes=[HS[r]], engine=PENG)
                                p.v("tensor_copy", hcur[r][:], hnew[r][:], reads=[hnew[r]], writes=[hcur[r]])
                        for r, dstd in enumerate((sr_s, si_s)):
                            nrow = NSS * 32
                            p.v("tensor_copy", hT[:, :].rearrange("q (b s) -> q b s", s=32), hcur[r][:, :, :].rearrange("q s b -> q b s"),
                                reads=[hcur[r]], writes=[hT])
                            for c0 in range(0, nrow, 128):
                                c1 = min(nrow, c0 + 128)
                                p.tr(PX[0:c1 - c0, 0:128], hT[:, c0:c1], identf, reads=[hT, cst], writes=[PX])
                                p.act(stl[0:c1 - c0, r, :], PX[0:c1 - c0, 0:128], AF.Copy, reads=[PX], writes=[stl])
                                p.dma(dstd[i, c0:c1, :], stl[0:c1 - c0, r, :], reads=[stl], writes=[DT["out"]])
                        for s in range(32):
                            y_mm(s, HS[0], HS[0][:, s, :], HS[1], HS[1][:, s, :])
                    for m in range(8):
                        pa = PA[m % 2]
                        for k in range(8):
                            p.mm(pa[:, 0:nt], Wg[:, k, m * 128:(m + 1) * 128], zT[:, k, 0:nt], k == 0, k == 7, reads=[Wg, zT], writes=[pa])
                        sg = sgb[m % 2]
                        p.act(sg[:, 0:nt], pa[:, 0:nt], AF.Sigmoid, reads=[pa], writes=[sg])
                        p.v("tensor_tensor", zzT[:, m, 0:nt], zT[:, m, 0:nt], sg[:, 0:nt], ALU.mult, reads=[zT, sg], writes=[zzT], engine=PENG2)
                    for s in range(t.nsub):
                        out_proj(t, s, zzT, lambda k, s, n: zzT[:, k, s * 128:s * 128 + n], Wo, 8)
                        post_norm_residual(t, s, hb, gpost, tb, ss2)
                    store_h(t, hb, False)
                p.barrier()

        def even_phase(l):
            i = l // 2
            lam_init = lam_init_fn(l)
            with ExitStack() as es:
                gpre = SB(es, "gpre", [128, D]); gpost = SB(es, "gpost", [128, D])
                KT = SB(es, "KT", [128, 4, SEQ], BF16)
                VE = SB(es, "VE", [128, NBLK, 4, 129], BF16)
                KTs = SB(es, "KTs", [128, 4, NST], BF16)
                qTs = SB(es, "qTs", [128, 4, NST], BF16)
                cTs = SB(es, "cTs", [128, 4, NST], BF16)
                VEs = SB(es, "VEs", [4, NSS, 4, 129], BF16)
                lam = SB(es, "lam", [128, 8])
                sgbc = SB(es, "sgbc", [128, 128])
                cw = SB(es, "cw", [128, 4, 31]); cbias = SB(es, "cbias", [128, 4]); lng = SB(es, "lng", [128, 4]); lnb = SB(es, "lnb", [128, 4])
                p.dma(gpre[:], bcast(g_h["g_mix_pre"], l * D, D), writes=[gpre])
                p.dma(gpost[:], bcast(g_h["g_mix_post"], l * D, D), writes=[gpost])
                p.v("memset", VE[:, :, :, 128:129], 1.0, writes=[VE])
                p.v("memset", VEs[:, :, :, 128:129], 1.0, writes=[VEs])
                with ExitStack() as es2:
                    lq = SB(es2, "lq", [128, 256]); lt = SB(es2, "lt", [128, 128])
                    p.dma(lq[:], bcast(lqk_h, i * 256, 256), writes=[lq])
                    p.v("tensor_tensor", lt[:, 0:64], lq[:, 0:64], lq[:, 64:128], ALU.mult, reads=[lq], writes=[lt])
                    p.v("tensor_tensor", lt[:, 64:128], lq[:, 128:192], lq[:, 192:256], ALU.mult, reads=[lq], writes=[lt])
                    p.v("tensor_reduce", lam[:, 0:2], lt[:, :].rearrange("q (a b) -> q a b", a=2), AX.X, ALU.add, reads=[lt], writes=[lam])
                    p.act(lam[:, 0:2], lam[:, 0:2], AF.Exp, reads=[lam], writes=[lam])
                    p.v("tensor_tensor", lam[:, 2:3], lam[:, 0:1], lam[:, 1:2], ALU.subtract, reads=[lam], writes=[lam])
                    p.v("tensor_scalar", lam[:, 3:4], lam[:, 2:3], float(lam_init), -1.0, ALU.add, ALU.mult, reads=[lam], writes=[lam])
                    p.v("scalar_tensor_tensor", lam[0:8, 4:5], cm1, lam[0:8, 3:4], cm0, ALU.mult, ALU.add, reads=[cst, lam], writes=[lam])
                    p.dma(sgbc[:], bcast(subln_h, i * 128, 128), writes=[sgbc])
                    p.v("tensor_scalar", sgbc[:], sgbc[:], float(1.0 - lam_init), None, ALU.mult, reads=[sgbc], writes=[sgbc])
                    cwl = SB(es2, "cwl", [31, 512])
                    p.dma(cwl[:], conv_w[i], writes=[cwl])
                    for m in range(4):
                        p.tr(PX[:, m * 32:m * 32 + 31], cwl[:, m * 128:(m + 1) * 128], identf[0:31, 0:31], reads=[cwl, cst], writes=[PX])
                    p.v("tensor_copy", cw[:], PX[:, 0:128].rearrange("q (m j) -> q m j", m=4)[:, :, 0:31], reads=[PX], writes=[cw])
                    p.dma(cbias[:], conv_b[i].rearrange("m q -> q m"), writes=[cbias], allow_slow_non_contiguous=True)
                    p.dma(lng[:], ln_g[i].rearrange("m q -> q m"), writes=[lng], allow_slow_non_contiguous=True)
                    p.dma(lnb[:], ln_b[i].rearrange("m q -> q m"), writes=[lnb], allow_slow_non_contiguous=True)
                    p.barrier()
                neglam = lam[:, 3:4]

                tiles = make_tiles()
                ptiles = [t for t in tiles if t.kind == "p"]
                stile = tiles[-1]
                for sq in range(NPS):
                    seq_tiles = [t for t in ptiles if t.seq == sq]
                    with ExitStack() as esA:
                        phaseA(l, i, esA, seq_tiles + ([stile] if sq == NPS - 1 else []), gpre, KT, VE, KTs, qTs, cTs, VEs, cw, cbias, lng, lnb)
                    with ExitStack() as esB:
                        phaseB(l, i, esB, seq_tiles, stile if sq == NPS - 1 else None, gpost, KT, VE, KTs, qTs, cTs, VEs, lam, neglam, sgbc)
                p.barrier()

        def phaseA(l, i, es, tiles, gpre, KT, VE, KTs, qTs, cTs, VEs, cw, cbias, lng, lnb):
            Win = SB(es, "WinE", [128, 8, 2560], BF16)
            load_weight(Win, w_in_even[i], 8, 2560)
            hbs = [SB(es, "hbA%d" % k, [128, 2, D]) for k in range(2)]
            xn = SB(es, "xnA", [128, 2, D], BF16)
            xT = SB(es, "xTA", [128, 8, TT], BF16)
            qst = SB(es, "qst", [128, 4, TT], BF16)
            cst_ = SB(es, "cstA", [128, 4, TT], BF16)
            kvst = [SB(es, "kvst%d" % k, [128, 512]) for k in range(2)]
            ss = SB(es, "ssA", [128, 4])
            Ub = SB(es, "Ub", [128, 4, 30 + TT])
            FULL = SB(es, "FULL", [128, 4, NSS, 34])
            acc = SB(es, "acc", [128, 4, TT])
            sq_ = SB(es, "sqA", [128, 4, TT])
            sgf = SB(es, "sgf", [128, TT])
            mu = SB(es, "mu", [128, TT]); rs = SB(es, "rs", [128, TT]); w3 = SB(es, "w3", [128, TT])
            utm = SB(es, "utm", [128, 512]); sgt = SB(es, "sgt", [128, 512])
            scl = SB(es, "scl", [120, 512])
            vbf = SB(es, "vbf", [NST, 512], BF16)
            for it, t in enumerate(tiles):
                hb = hbs[it % 2]
                nt = t.ntok; n = t.np
                isp = t.kind == "p"
                load_h(t, hb)
                norm_to_xT(t, hb, gpre, xn, xT, ss)
                for hh in range(4):
                    pa = PA[hh % 2]
                    for k in range(8):
                        p.mm(pa[:, 0:nt], Win[:, k, hh * 128:(hh + 1) * 128], xT[:, k, 0:nt], k == 0, k == 7, reads=[Win, xT], writes=[pa])
                    if isp:
                        p.act(qst[:, hh, 0:nt], pa[:, 0:nt], AF.Copy, reads=[pa], writes=[qst])
                    else:
                        p.act(qTs[:, hh, :], pa[:, 0:nt], AF.Copy, reads=[pa], writes=[qTs])
                for hh in range(4):
                    pa = PA[hh % 2]
                    for k in range(8):
                        p.mm(pa[:, 0:nt], Win[:, k, 512 + hh * 128:512 + (hh + 1) * 128], xT[:, k, 0:nt], k == 0, k == 7, reads=[Win, xT], writes=[pa])
                    if isp:
                        p.act(KT[:, hh, t.ti * TT:t.ti * TT + nt], pa[:, 0:nt], AF.Copy, reads=[pa], writes=[KT])
                    else:
                        p.act(KTs[:, hh, :], pa[:, 0:nt], AF.Copy, reads=[pa], writes=[KTs])
                if isp:
                    p.dma(QT[:, :, t.ti * TT:t.ti * TT + nt], qst[:, :, 0:nt], reads=[qst], writes=[DT["QT"]])
                for s in range(t.nsub):
                    for which, c0 in ((0, 512), (1, 1024)):
                        pb = PB[which]
                        for k in range(8):
                            p.mm(pb[0:n, :], xT[:, k, s * 128:s * 128 + n], Win[:, k, c0:c0 + 512], k == 0, k == 7, reads=[xT, Win], writes=[pb])
                        st_ = kvst[which]
                        p.act(st_[0:n, :], pb[0:n, :], AF.Copy, reads=[pb], writes=[st_])
                        if isp:
                            dst = (nk_p, nv_p)[which][i, t.row0 + s * 128:t.row0 + s * 128 + n, :]
                        else:
                            dst = (nk_s, nv_s)[which][i, 0:n, :]
                        p.dma(dst, st_[0:n, :], reads=[st_], writes=[DT["out"]])
                        if which == 1:
                            if isp:
                                blk = t.ti * (TT // 128) + s
                                p.v("tensor_copy", VE[:, blk, :, 0:128], pb[:, :].rearrange("q (h e) -> q h e", h=4), reads=[pb], writes=[VE])
                            else:
                                p.v("tensor_copy", vbf[:, :], pb[0:n, :], reads=[pb], writes=[vbf])
                                for b in range(NSS):
                                    p.dma(VEs[0:4, b, :, 0:128], vbf[4 * b:4 * b + 4, :].rearrange("q (h e) -> q h e", h=4), reads=[vbf], writes=[VEs])
                if isp and t.first:
                    p.v("memset", Ub[:, :, 0:30], 0.0, writes=[Ub])
                if not isp:
                    for b0 in range(0, NSS, 4):
                        nb = min(4, NSS - b0)
                        p.dma(scl[0:nb * 30, :], sconv[i, b0:b0 + nb].rearrange("b r c -> (b r) c"), writes=[scl])
                        for m in range(4):
                            p.tr(PX[:, m * 128:m * 128 + nb * 30], scl[0:nb * 30, m * 128:(m + 1) * 128], identf[0:nb * 30, 0:nb * 30],
                                 reads=[scl, cst], writes=[PX])
                        p.v("tensor_copy", FULL[:, :, b0:b0 + nb, 0:30],
                            PX[:, :].rearrange("q (m x) -> q m x", m=4)[:, :, 0:nb * 30].rearrange("q m (b r) -> q m b r", r=30), reads=[PX], writes=[FULL])
                        for b in range(b0, b0 + nb):
                            p.dma(ncv_s[i, b, 0:26, :], sconv[i, b, 4:30, :], writes=[DT["out"]])
                for m in range(4):
                    for k in range(8):
                        p.mm(PA[0][:, 0:nt], Win[:, k, 1536 + m * 128:1536 + (m + 1) * 128], xT[:, k, 0:nt], k == 0, k == 7, reads=[Win, xT], writes=[PA[0]])
                    for k in range(8):
                        p.mm(PA[1][:, 0:nt], Win[:, k, 2048 + m * 128:2048 + (m + 1) * 128], xT[:, k, 0:nt], k == 0, k == 7, reads=[Win, xT], writes=[PA[1]])
                    p.act(sgf[:, 0:nt], PA[1][:, 0:nt], AF.Sigmoid, reads=[PA[1]], writes=[sgf])
                    if isp:
                        p.v("tensor_tensor", Ub[:, m, 30:30 + nt], PA[0][:, 0:nt], sgf[:, 0:nt], ALU.mult, reads=[PA[0], sgf], writes=[Ub])
                    else:
                        p.v("tensor_tensor", FULL[:, m, :, 30:34], PA[0][:, 0:nt].rearrange("q (b t) -> q b t", t=4),
                            sgf[:, 0:nt].rearrange("q (b t) -> q b t", t=4), ALU.mult, reads=[PA[0], sgf], writes=[FULL])
                for m in range(4):
                    if isp:
                        src = lambda j: Ub[:, m, j:j + nt]
                        srcb = Ub
                        av = acc[:, m, 0:nt]
                    else:
                        src = lambda j: FULL[:, m, :, j:j + 4]
                        srcb = FULL
                        av = acc[:, m, 0:nt].rearrange("q (b t) -> q b t", t=4)
                    p.v("tensor_scalar", av, src(0), cw[:, m, 0:1], cbias[:, m:m + 1], ALU.mult, ALU.add, reads=[srcb, cw, cbias], writes=[acc])
                    for j in range(1, 31):
                        p.v("scalar_tensor_tensor", av, src(j), cw[:, m, j:j + 1], av, ALU.mult, ALU.add, reads=[srcb, cw, acc], writes=[acc])
                    p.act(sq_[:, m, 0:nt], acc[:, m, 0:nt], AF.Square, reads=[acc], writes=[sq_])
                if isp and not t.last:
                    p.v("tensor_copy", Ub[:, :, 0:30], Ub[:, :, nt:nt + 30], reads=[Ub], writes=[Ub], engine=PENG)
                for m in range(4):
                    p.mm(PA[0][:, 0:nt], onesf[:, :], acc[:, m, 0:nt], m == 0, m == 3, reads=[onesf, acc], writes=[PA[0]])
                for m in range(4):
                    p.mm(PA[1][:, 0:nt], onesf[:, :], sq_[:, m, 0:nt], m == 0, m == 3, reads=[onesf, sq_], writes=[PA[1]])
                p.v("tensor_copy", mu[:, 0:nt], PA[0][:, 0:nt], reads=[PA[0]], writes=[mu])
                p.v("tensor_tensor", w3[:, 0:nt], mu[:, 0:nt], mu[:, 0:nt], ALU.mult, reads=[mu], writes=[w3])
                p.v("tensor_tensor", rs[:, 0:nt], PA[1][:, 0:nt], w3[:, 0:nt], ALU.subtract, reads=[PA[1], w3], writes=[rs])
                rstd_calc(rs[:, 0:nt], rs[:, 0:nt], 1.0, 1e-5, [rs])
                for m in range(4):
                    p.v("tensor_tensor", w3[:, 0:nt], acc[:, m, 0:nt], mu[:, 0:nt], ALU.subtract, reads=[acc, mu], writes=[w3])
                    p.v("tensor_tensor", w3[:, 0:nt], w3[:, 0:nt], rs[:, 0:nt], ALU.mult, reads=[w3, rs], writes=[w3])
                    dstc = cst_[:, m, 0:nt] if isp else cTs[:, m, :]
                    p.act(dstc, w3[:, 0:nt], AF.Silu, reads=[w3, lng, lnb], writes=[cst_ if isp else cTs], scale=lng[:, m:m + 1], bias=lnb[:, m:m + 1])
                if isp:
                    p.dma(CT[:, :, t.ti * TT:t.ti * TT + nt], cst_[:, :, 0:nt], reads=[cst_], writes=[DT["CT"]])
                if t.last:
                    s = t.nsub - 1
                    for which, c0 in ((0, 1536), (1, 2048)):
                        for k in range(8):
                            p.mm(PB[which][0:n, :], xT[:, k, s * 128:s * 128 + n], Win[:, k, c0:c0 + 512], k == 0, k == 7, reads=[xT, Win], writes=[PB[which]])
                    p.act(sgt[0:n, :], PB[1][0:n, :], AF.Sigmoid, reads=[PB[1]], writes=[sgt])
                    p.v("tensor_tensor", utm[0:n, :], PB[0][0:n, :], sgt[0:n, :], ALU.mult, reads=[PB[0], sgt], writes=[utm])
                    if isp:
                        p.dma(ncv_p[i, t.seq, :, :], utm[98:128, :], reads=[utm], writes=[DT["out"]])
                    else:
                        for b in range(NSS):
                            p.dma(ncv_s[i, b, 26:30, :], utm[4 * b:4 * b + 4, :], reads=[utm], writes=[DT["out"]])
            p.barrier()

        def phaseB(l, i, es, ptiles, stile, gpost, KT, VE, KTs, qTs, cTs, VEs, lam, neglam, sgbc):
            Wo = SB(es, "WoE", [128, 8, D], BF16)
            load_weight(Wo, w_out_even[i], 8, D)
            hbs = [SB(es, "hbB%d" % k, [128, 2, D]) for k in range(2)]
            qTt = [SB(es, "qTt%d" % k, [128, 4, TT], BF16) for k in range(2)]
            mixT = [SB(es, "mixT%d" % k, [128, 8, TT], BF16) for k in range(2)]
            Pb = [SB(es, "Pb%d" % k, [128, TT], BF16) for k in range(2)]
            tb = SB(es, "tbB", [128, D])
            ss2 = SB(es, "ss2B", [128, 4])
            at = SB(es, "at", [128, 128]); at2 = SB(es, "at2", [128, 128]); an = SB(es, "an", [128, 128], BF16)
            sm = SB(es, "smB", [128, 16])
            for it, t in enumerate(ptiles):
                hb = hbs[it % 2]; qt_ = qTt[it % 2]; mx = mixT[it % 2]
                t0 = t.ti * TT
                load_h(t, hb)
                p.dma(qt_[:], QT[:, :, t0:t0 + TT], reads=[DT["QT"]], writes=[qt_])
                p.dma(mx[:, 4:8, :], CT[:, :, t0:t0 + TT], reads=[DT["CT"]], writes=[mx])
                nsub = TT // 128
                kb0 = t.ti * nsub
                nkb = kb0 + nsub
                for hh in range(4):
                    for c in range(2):
                        for kb in range(nkb):
                            j = kb - kb0
                            q0 = 128 * j if j > 0 else 0
                            nn = TT - q0
                            ps = PA[kb % 2]; pbuf = Pb[kb % 2]
                            p.mm(ps[:, 0:nn], KT[64 * c:64 * c + 64, hh, kb * 128:(kb + 1) * 128], qt_[64 * c:64 * c + 64, hh, q0:TT], True, True,
                                 reads=[KT, qt_], writes=[ps])
                            p.act(pbuf[:, 0:nn], ps[:, 0:nn], AF.Exp, reads=[ps], writes=[pbuf], scale=0.125)
                            if j >= 0:
                                p.v("tensor_tensor", pbuf[:, 0:128], pbuf[:, 0:128], trib[:, :], ALU.mult, reads=[pbuf, trib], writes=[pbuf], engine=PENG)
                            for sub in range(max(j, 0), nsub):
                                off = sub * 128 - q0
                                po = PO[c]
                                p.mm(po[:, sub * 129:sub * 129 + 129], pbuf[:, off:off + 128], VE[:, kb, hh, :], kb == 0 and sub == 0, kb == kb0 + sub,
                                     reads=[pbuf, VE], writes=[po], skip_group_check=True)
                    for sub in range(nsub):
                        o1 = PO[0][:, sub * 129:sub * 129 + 128]; l1 = PO[0][:, sub * 129 + 128:sub * 129 + 129]
                        o2 = PO[1][:, sub * 129:sub * 129 + 128]; l2 = PO[1][:, sub * 129 + 128:sub * 129 + 129]
                        p.v("reciprocal", sm[:, 0:1], l1, reads=[PO[0]], writes=[sm])
                        p.v("reciprocal", sm[:, 1:2], l2, reads=[PO[1]], writes=[sm])
                        p.v("tensor_tensor", sm[:, 1:2], sm[:, 1:2], neglam, ALU.mult, reads=[sm, lam], writes=[sm])
                        p.v("tensor_scalar", at[:], o1, sm[:, 0:1], None, ALU.mult, reads=[PO[0], sm], writes=[at])
                        p.v("scalar_tensor_tensor", at2[:], o2, sm[:, 1:2], at[:], ALU.mult, ALU.add, reads=[PO[1], sm, at], writes=[at2])
                        p.act(at[:], at2[:], AF.Square, reads=[at2], writes=[at, sm], accum_out=sm[:, 2:3])
                        rstd_calc(sm[:, 2:3], sm[:, 2:3], 128, 1e-5, [sm])
                        p.v("scalar_tensor_tensor", an[:], at2[:], sm[:, 2:3], sgbc[:], ALU.mult, ALU.mult, reads=[at2, sm, sgbc], writes=[an])
                        p.tr(PT[:, 0:128], an[:], identb[:], reads=[an, identb], writes=[PT])
                        p.v("tensor_copy", mx[:, hh, sub * 128:(sub + 1) * 128], PT[:, 0:128], reads=[PT], writes=[mx])
                for s in range(nsub):
                    out_proj(t, s, mx, lambda k, s, n: mx[:, k, s * 128:s * 128 + n], Wo, 8)
                    post_norm_residual(t, s, hb, gpost, tb, ss2)
                store_h(t, hb, False)
            if stile is not None:
                t = stile
                hb = hbs[0]; mx = mixT[0]
                load_h(t, hb)
                p.v("tensor_copy", mx[:, 4:8, 0:NST], cTs[:, :, :], reads=[cTs], writes=[mx], engine=PENG)
                Qb = SB(es, "Qb", [128, 4, 8], BF16)
                Kpg = [SB(es, "Kpg%d" % k, [128, 512], BF16) for k in range(3)]
                Vpg = [SB(es, "Vpg%d" % k, [128, 4, 129], BF16) for k in range(3)]
                Vraw = [SB(es, "Vraw%d" % k, [128, 512], BF16) for k in range(3)]
                KTp = [SB(es, "KTp%d" % k, [128, 512], BF16) for k in range(2)]
                Pp = [SB(es, "Pp%d" % k, [128, 32], BF16) for k in range(2)]
                Osb = SB(es, "Osb", [8, 4, 129])
                Rr = SB(es, "Rr", [8, 8]); SelR = SB(es, "SelR", [8, 4, 4])
                ats = SB(es, "ats", [4, 512]); ats2 = SB(es, "ats2", [4, 512]); ans = SB(es, "ans", [4, 512], BF16)
                sms = SB(es, "sms", [4, 8])
                m4b = SB(es, "m4b", [4, 32], BF16)
                p.v("tensor_copy", m4b[:], mask4_f, reads=[cst], writes=[m4b])
                for k in range(3):
                    p.v("memset", Vpg[k][:, :, 128:129], 1.0, writes=[Vpg[k]])
                p.v("memset", Qb[:], 0.0, writes=[Qb])
                pg = 0
                for b in range(NSS):
                    p.v("tensor_copy", Qb[0:64, :, 0:4], qTs[0:64, :, 4 * b:4 * b + 4], reads=[qTs], writes=[Qb])
                    p.v("tensor_copy", Qb[64:128, :, 4:8], qTs[64:128, :, 4 * b:4 * b + 4], reads=[qTs], writes=[Qb])
                    for n_ in range(NPAGES + 1):
                        newblk = n_ == NPAGES
                        pps = PX; ppb = Pp[n_ % 2]
                        if not newblk:
                            kp = Kpg[pg % 3]; vp = Vpg[pg % 3]; ktp = KTp[pg % 2]
                            pg += 1
                            col = b * NPAGES + n_
                            p.emit("pool", lambda e, kp=kp, col=col: e.indirect_dma_start(
                                out=kp[:, :], out_offset=None, in_=ck[:, :],
                                in_offset=bass.IndirectOffsetOnAxis(ap=gidx[:, i, col:col + 1], axis=0)), reads=[gidx], writes=[kp], dma=True)
                            vr = Vraw[(pg - 1) % 3]
                            p.emit("pool", lambda e, vr=vr, col=col: e.indirect_dma_start(
                                out=vr[:, :], out_offset=None, in_=cv[:, :],
                                in_offset=bass.IndirectOffsetOnAxis(ap=gidx[:, i, col:col + 1], axis=0)), reads=[gidx], writes=[vr], dma=True)
                            p.v("tensor_copy", vp[:, :, 0:128], vr[:, :].rearrange("q (h e) -> q h e", h=4), reads=[vr], writes=[vp])
                            for hh in range(4):
                                p.tr(PT[:, hh * 128:(hh + 1) * 128], kp[:, hh * 128:(hh + 1) * 128], identb[:], reads=[kp, identb], writes=[PT], inc=(hh == 3))
                            p.v("tensor_copy", ktp[:, :], PT[:, 0:512], reads=[PT], writes=[ktp])
                            for hh in range(4):
                                p.mm(pps[:, hh * 8:hh * 8 + 8], ktp[:, hh * 128:(hh + 1) * 128], Qb[:, hh, :], True, True, reads=[ktp, Qb], writes=[pps], inc=(hh == 3))
                            p.act(ppb[:, :], pps[:, 0:32], AF.Exp, reads=[pps], writes=[ppb], scale=0.125)
                            for hh in range(4):
                                po = PO[hh // 2]
                                p.mm(po[0:8, (hh % 2) * 129:(hh % 2) * 129 + 129], ppb[:, hh * 8:hh * 8 + 8], vp[:, hh, :], n_ == 0 and hh % 2 == 0, False,
                                     reads=[ppb, vp], writes=[po], inc=False, skip_group_check=True)
                        else:
                            for hh in range(4):
                                p.mm(pps[0:4, hh * 8:hh * 8 + 8], KTs[:, hh, 4 * b:4 * b + 4], Qb[:, hh, :], True, True, reads=[KTs, Qb], writes=[pps], inc=(hh == 3))
                            p.act(ppb[0:4, :], pps[0:4, 0:32], AF.Exp, reads=[pps], writes=[ppb], scale=0.125)
                            p.v("tensor_tensor", ppb[0:4, :], ppb[0:4, :], m4b[:, :], ALU.mult, reads=[ppb, m4b], writes=[ppb])
                            for hh in range(4):
                                po = PO[hh // 2]
                                p.mm(po[0:8, (hh % 2) * 129:(hh % 2) * 129 + 129], ppb[0:4, hh * 8:hh * 8 + 8], VEs[0:4, b, hh, :], False, True,
                                     reads=[ppb, VEs], writes=[po], inc=True, skip_group_check=True)
                    for hp in range(2):
                        p.act(Osb[:, 2 * hp:2 * hp + 2, :], PO[hp][0:8, 0:258].rearrange("q (h e) -> q h e", h=2), AF.Copy, reads=[PO[hp]], writes=[Osb])
                    p.v("reciprocal", Rr[:, 0:4], Osb[:, :, 128], reads=[Osb], writes=[Rr])
                    p.v("tensor_scalar", Rr[:, 0:4], Rr[:, 0:4], lam[0:8, 4:5], None, ALU.mult, reads=[Rr, lam], writes=[Rr])
                    for hh in range(4):
                        p.v("tensor_scalar", SelR[:, hh, :], sel_f, Rr[:, hh:hh + 1], None, ALU.mult, reads=[cst, Rr], writes=[SelR])
                    for hh in range(4):
                        p.mm(PB[0][0:4, hh * 128:(hh + 1) * 128], SelR[:, hh, :], Osb[:, hh, 0:128], True, True, reads=[SelR, Osb], writes=[PB[0]], inc=(hh == 3))
                    p.v("tensor_copy", ats[:], PB[0][0:4, :], reads=[PB[0]], writes=[ats])
                    for hh in range(4):
                        p.act(ats2[:, hh * 128:(hh + 1) * 128], ats[:, hh * 128:(hh + 1) * 128], AF.Square, reads=[ats], writes=[ats2, sms],
                              accum_out=sms[:, hh:hh + 1])
                    rstd_calc(sms[:, 0:4], sms[:, 0:4], 128, 1e-5, [sms])
                    for hh in range(4):
                        p.v("scalar_tensor_tensor", ans[:, hh * 128:(hh + 1) * 128], ats[:, hh * 128:(hh + 1) * 128], sms[:, hh:hh + 1], sgbc[0:4, :],
                            ALU.mult, ALU.mult, reads=[ats, sms, sgbc], writes=[ans])
                    for hh in range(4):
                        p.tr(PT[:, hh * 4:hh * 4 + 4], ans[:, hh * 128:(hh + 1) * 128], identb[0:4, 0:4], reads=[ans, identb], writes=[PT], inc=(hh == 3))
                    p.v("tensor_copy", mx[:, 0:4, 4 * b:4 * b + 4], PT[:, 0:16].rearrange("q (h t) -> q h t", h=4), reads=[PT], writes=[mx])
                out_proj(t, 0, mx, lambda k, s, n: mx[:, k, 0:n], Wo, 8)
                post_norm_residual(t, 0, hb, gpost, tb, ss2)
                store_h(t, hb, False)
            p.barrier()

        only = cfg.get("ONLY", ("even", "odd", "ffn"))
        for l in range(DEPTH):
            if l % 2 == 0:
                if "even" in only:
                    even_phase(l)
            else:
                if "odd" in only:
                    odd_phase(l)
            if "ffn" in only:
                ffn_phase(l, l == DEPTH - 1)
        p.final_wait()
        n_ins = p.n_ins
    return nc, n_ins


def make_consts():
    c = np.zeros((128, 512), np.float32)
    c[:, 0:128] = np.eye(128, dtype=np.float32)
    c[:, 128] = np.arange(128, dtype=np.float32)
    kk = np.arange(128)[:, None]; qq = np.arange(128)[None, :]
    c[:, 129:257] = (kk <= qq).astype(np.float32)
    m4 = (np.arange(4)[:, None] <= np.arange(4)[None, :]).astype(np.float32)
    c[0:4, 257:289] = np.tile(m4, (1, 8))
    c[0:4, 289] = 1.0
    c[4:8, 290] = 1.0
    c[0:8, 291:295] = np.tile(np.eye(4, dtype=np.float32), (2, 1))
    return c


def run(cfg, inputs, trace=False):
    NC, NB, SEQ, DEC_B, NPAGES, NPHYS, DEPTH = (cfg[k] for k in ("NC", "NB", "SEQ", "DEC_B", "NPAGES", "NPHYS", "DEPTH"))
    NPS = NB // NC; NSS = DEC_B // NC; NST = NSS * 4
    NE = (DEPTH + 1) // 2; NO = DEPTH // 2
    nc, n_ins = build(cfg)
    f = lambda a: np.ascontiguousarray(np.asarray(a, dtype=np.float32))
    I = inputs
    shared = {
        "ck": f(I["cache_k"]).reshape(NE * NPHYS * 128, 512),
        "cv": f(I["cache_v"]).reshape(NE * NPHYS * 128, 512),
        "g_mix_pre": f(I["g_mix_pre"]), "g_mix_post": f(I["g_mix_post"]), "g_ffn_pre": f(I["g_ffn_pre"]), "g_ffn_post": f(I["g_ffn_post"]),
        "w_in_even": f(I["w_in_even"]), "lambda_qk": f(I["lambda_qk"]).reshape(NE, 256), "subln_g": f(I["subln_g"]),
        "conv_w": f(I["conv_w"]), "conv_b": f(I["conv_b"]).reshape(NE, 4, 128), "conv_ln_g": f(I["conv_ln_g"]).reshape(NE, 4, 128),
        "conv_ln_b": f(I["conv_ln_b"]).reshape(NE, 4, 128), "w_out_even": f(I["w_out_even"]),
        "w_in_odd": f(I["w_in_odd"]), "ssm_a_re": f(I["ssm_a_re"]).reshape(NO, 32, 128), "ssm_a_im": f(I["ssm_a_im"]).reshape(NO, 32, 128),
        "ssm_log_dt_rep": np.ascontiguousarray(np.repeat(f(I["ssm_log_dt"])[:, :, None], 64, axis=2)).reshape(NO, 32, 128),
        "ssm_b_re": f(I["ssm_b_re"]), "ssm_b_im": f(I["ssm_b_im"]), "ssm_c_re": f(I["ssm_c_re"]), "ssm_c_im": f(I["ssm_c_im"]),
        "ssm_d": f(I["ssm_d"]).reshape(NO, 8, 128), "w_gate_odd": f(I["w_gate_odd"]), "w_out_odd": f(I["w_out_odd"]),
        "w_ffn_up": f(I["w_ffn_up"]), "w_ffn_down": f(I["w_ffn_down"]), "consts": make_consts(),
    }
    xpf = f(I["x_prompt"]); xsf = f(I["x_sample"])
    pt = np.asarray(I["page_table"], dtype=np.int32)
    scv = f(I["state_conv"]); sr = f(I["state_ssm_re"]); si = f(I["state_ssm_im"])
    in_maps = []
    for c in range(NC):
        m = dict(shared)
        m["xp"] = np.ascontiguousarray(xpf[c * NPS:(c + 1) * NPS].reshape(NPS * SEQ, D))
        m["xs"] = np.ascontiguousarray(xsf[c * NSS:(c + 1) * NSS].reshape(NST, D))
        m["pt"] = np.ascontiguousarray(pt[c * NSS:(c + 1) * NSS].reshape(1, NSS * NPAGES))
        m["sconv"] = np.ascontiguousarray(scv[:, c * NSS:(c + 1) * NSS])
        m["sre"] = np.ascontiguousarray(sr[:, c * NSS:(c + 1) * NSS].reshape(NO, NSS * 32, 128))
        m["sim"] = np.ascontiguousarray(si[:, c * NSS:(c + 1) * NSS].reshape(NO, NSS * 32, 128))
        in_maps.append(m)
    res = run_bass_kernel_spmd(nc, in_maps, core_ids=list(range(NC)), **({"trace": True} if trace else {}))
    R = res.results
    cat = lambda name, ax: np.concatenate([R[c][name] for c in range(NC)], axis=ax)
    y_p = cat("y_p", 0).reshape(NB, SEQ, D)
    y_s = cat("y_s", 0).reshape(DEC_B, 4, D)
    nk_p = cat("nk_p", 1).reshape(NE, NB, SEQ, 4, 2, 64)
    nv_p = cat("nv_p", 1).reshape(NE, NB, SEQ, 4, 128)
    nk_s = cat("nk_s", 1).reshape(NE, DEC_B, 4, 4, 2, 64)
    nv_s = cat("nv_s", 1).reshape(NE, DEC_B, 4, 4, 128)
    ncv_p = cat("ncv_p", 1)
    ncv_s = cat("ncv_s", 1)
    sr_p = cat("sr_p", 1).reshape(NO, NB, 64, 64)
    si_p = cat("si_p", 1).reshape(NO, NB, 64, 64)
    sr_s = cat("sr_s", 1).reshape(NO, DEC_B, 64, 64)
    si_s = cat("si_s", 1).reshape(NO, DEC_B, 64, 64)
    outs = (y_p, y_s, nk_p, nv_p, nk_s, nv_s, ncv_p, ncv_s, sr_p, si_p, sr_s, si_s)
    return tuple(np.ascontiguousarray(o, dtype=np.float32) for o in outs), res


def kernel(**inputs):
    outs, _ = run(FULL_CFG, inputs)
    return outs
```
